# Optimizing a Trainium2 kernel written in Bass

```python
import math, functools
import jax, jax.numpy as jnp
from jax import lax
import numpy as np

D_MODEL = 1024
BATCH = 2
SEQ = 8192
DEPTH = 2
DEC_BATCH = 128
DEC_SEQ = 8
PAST_LEN = 2048
PAGE_SIZE = 128

SSM_WIDTH = D_MODEL // 2
SSM_GROUP = 16
SSM_GROUPS = SSM_WIDTH // SSM_GROUP
SSM_STATE = 64
HEAD_DIM = 64
ATT_WIDTH = D_MODEL - SSM_WIDTH
N_HEADS = ATT_WIDTH // HEAD_DIM
IDX_HEADS = 8
IDX_DIM = 32
TOPK_MAX = 256
Q_BLOCK = 128
D_FF = 2816
D_IN = SSM_WIDTH + 3 * ATT_WIDTH + IDX_HEADS * IDX_DIM + IDX_DIM + IDX_HEADS
LN_EPS = 1e-5
DT_MIN = 1e-3
DT_MAX = 1e-1

kernel_name = "hymba_s5_dsa_macaron_deepnorm_step"


def layer_norm(x, g, b):
    xf = x.astype(jnp.float32)
    mu = jnp.mean(xf, axis=-1, keepdims=True)
    var = jnp.mean(jnp.square(xf - mu), axis=-1, keepdims=True)
    return ((xf - mu) * lax.rsqrt(var + LN_EPS) * g.astype(jnp.float32) + b.astype(jnp.float32)).astype(x.dtype)


def swiglu(x, w_up, w_down):
    gate, up = jnp.split(x @ w_up, 2, axis=-1)
    return (jax.nn.silu(gate) * up) @ w_down


def s5_mixer(u, h0_re, h0_im, p):
    f32 = jnp.float32
    uf = u.astype(f32)
    lam_re = p['ssm_lambda_re'].astype(f32)
    lam_im = p['ssm_lambda_im'].astype(f32)
    step = jnp.exp(p['ssm_log_step'].astype(f32))[:, None]
    mag = jnp.exp(lam_re * step)
    ang = lam_im * step
    ab_re, ab_im = mag * jnp.cos(ang), mag * jnp.sin(ang)
    nr, ni = ab_re - 1.0, ab_im
    den = lam_re * lam_re + lam_im * lam_im
    f_re = (nr * lam_re + ni * lam_im) / den
    f_im = (ni * lam_re - nr * lam_im) / den
    b_re = p['ssm_b_re'].astype(f32)
    b_im = p['ssm_b_im'].astype(f32)
    bb_re = f_re[..., None] * b_re - f_im[..., None] * b_im
    bb_im = f_re[..., None] * b_im + f_im[..., None] * b_re
    bu_re = jnp.einsum('blgh,gph->blgp', uf, bb_re)
    bu_im = jnp.einsum('blgh,gph->blgp', uf, bb_im)
    h0r = h0_re.astype(f32)
    h0i = h0_im.astype(f32)
    bu_re = bu_re.at[:, 0].add(ab_re * h0r - ab_im * h0i)
    bu_im = bu_im.at[:, 0].add(ab_re * h0i + ab_im * h0r)
    a_re = jnp.broadcast_to(ab_re, bu_re.shape)
    a_im = jnp.broadcast_to(ab_im, bu_im.shape)

    def combine(e1, e2):
        a1r, a1i, b1r, b1i = e1
        a2r, a2i, b2r, b2i = e2
        return (a1r * a2r - a1i * a2i,
                a1r * a2i + a1i * a2r,
                a2r * b1r - a2i * b1i + b2r,
                a2r * b1i + a2i * b1r + b2i)

    _, _, h_re, h_im = lax.associative_scan(combine, (a_re, a_im, bu_re, bu_im), axis=1)
    y = (jnp.einsum('blgp,ghp->blgh', h_re, p['ssm_c_re'].astype(f32))
         - jnp.einsum('blgp,ghp->blgh', h_im, p['ssm_c_im'].astype(f32))
         + p['ssm_d'].astype(f32) * uf)
    return y.astype(u.dtype), h_re[:, -1], h_im[:, -1]


def gather_rows(rows, ids):
    return jax.vmap(lambda r, i: r[i])(rows, ids)


def index_topk(q_idx, w_idx, k_idx, qpos, topk):
    s = jax.nn.relu(jnp.einsum('bqhd,bsd->bqhs', q_idx, k_idx).astype(jnp.float32))
    score = jnp.einsum('bqh,bqhs->bqs', w_idx.astype(jnp.float32), s)
    kpos = jnp.arange(k_idx.shape[1])
    causal = kpos[None, :] <= qpos[:, None]
    score = jnp.where(causal[None], score, -jnp.inf)
    _, idx = lax.top_k(score, topk)
    valid = idx <= qpos[None, :, None]
    return idx, valid


def attend_selected(q, k_sel, v_sel, valid):
    logits = jnp.einsum('bqhd,bqkhd->bqhk', q, k_sel).astype(jnp.float32) * (HEAD_DIM ** -0.5)
    logits = jnp.where(valid[:, :, None, :], logits, -jnp.inf)
    probs = jax.nn.softmax(logits, axis=-1).astype(v_sel.dtype)
    return jnp.einsum('bqhk,bqkhd->bqhd', probs, v_sel)


def prompt_sparse_attention(q, k, v, q_idx, k_idx, w_idx):
    B, L = q.shape[0], q.shape[1]
    topk = min(TOPK_MAX, L // 4)
    nb = L // Q_BLOCK

    def to_blocks(t):
        return jnp.moveaxis(t.reshape(B, nb, Q_BLOCK, *t.shape[2:]), 1, 0)

    def one_block(args):
        blk, qb, qib, wb = args
        qpos = blk * Q_BLOCK + jnp.arange(Q_BLOCK)
        idx, valid = index_topk(qib, wb, k_idx, qpos, topk)
        return attend_selected(qb, gather_rows(k, idx), gather_rows(v, idx), valid)

    out = lax.map(one_block, (jnp.arange(nb), to_blocks(q), to_blocks(q_idx), to_blocks(w_idx)))
    return jnp.moveaxis(out, 0, 1).reshape(B, L, N_HEADS, HEAD_DIM)


def sample_sparse_attention(q, k_new, v_new, q_idx, k_idx_new, w_idx, cache_k_l, cache_v_l, cache_kidx_l, page_table):
    DB, S = q.shape[0], q.shape[1]
    n_pages = page_table.shape[1]
    past = n_pages * PAGE_SIZE
    topk = min(TOPK_MAX, (past + S) // 4)
    k_idx_past = cache_kidx_l[page_table].reshape(DB, past, IDX_DIM)
    k_idx_all = jnp.concatenate([k_idx_past, k_idx_new.astype(k_idx_past.dtype)], axis=1)
    qpos = past + jnp.arange(S)
    idx, valid = index_topk(q_idx, w_idx, k_idx_all, qpos, topk)
    in_past = idx < past
    pidx = jnp.minimum(idx, past - 1)
    phys = jnp.take_along_axis(page_table, (pidx // PAGE_SIZE).reshape(DB, -1), axis=1).reshape(idx.shape)
    flat = phys * PAGE_SIZE + pidx % PAGE_SIZE
    nidx = jnp.clip(idx - past, 0, S - 1)
    k_pool = cache_k_l.reshape(-1, N_HEADS, HEAD_DIM)
    v_pool = cache_v_l.reshape(-1, N_HEADS, HEAD_DIM)

    def per_query(t):
        return jnp.moveaxis(t[:, :, None], 1, 0)

    def one_query(args):
        qi, fi, ni, pi, vi = args
        mask = pi[..., None, None]
        ks = jnp.where(mask, k_pool[fi], gather_rows(k_new, ni).astype(k_pool.dtype))
        vs = jnp.where(mask, v_pool[fi], gather_rows(v_new, ni).astype(v_pool.dtype))
        return attend_selected(qi, ks.astype(qi.dtype), vs.astype(qi.dtype), vi)

    out = lax.map(one_query, (per_query(q), per_query(flat), per_query(nidx), per_query(in_past), per_query(valid)))
    return jnp.moveaxis(out, 0, 1).reshape(DB, S, N_HEADS, HEAD_DIM)


def decoder_layer(x, h0_re, h0_im, p, attention_fn):
    alpha = (2.0 * DEPTH) ** 0.25
    lead = x.shape[:2]
    x = layer_norm(alpha * x + 0.5 * swiglu(x, p['ffn1_up'], p['ffn1_down']), p['ln_g'][0], p['ln_b'][0])
    offs = np.cumsum([SSM_WIDTH, ATT_WIDTH, ATT_WIDTH, ATT_WIDTH, IDX_HEADS * IDX_DIM, IDX_DIM]).tolist()
    u, q, k, v, qi, ki, wi = jnp.split(x @ p['w_in'], offs, axis=-1)
    y_ssm, h_re, h_im = s5_mixer(u.reshape(*lead, SSM_GROUPS, SSM_GROUP), h0_re, h0_im, p)
    g = jax.nn.gelu(y_ssm.reshape(*lead, SSM_WIDTH))
    y_ssm = g * jax.nn.sigmoid(g @ p['w_glu'] + p['b_glu'])
    q = q.reshape(*lead, N_HEADS, HEAD_DIM)
    k = k.reshape(*lead, N_HEADS, HEAD_DIM)
    v = v.reshape(*lead, N_HEADS, HEAD_DIM)
    qi = qi.reshape(*lead, IDX_HEADS, IDX_DIM)
    y_att = attention_fn(q, k, v, qi, ki, wi).reshape(*lead, ATT_WIDTH)
    mix = jnp.concatenate([y_ssm, y_att.astype(y_ssm.dtype)], axis=-1) @ p['w_out']
    x = layer_norm(alpha * x + mix, p['ln_g'][1], p['ln_b'][1])
    x = layer_norm(alpha * x + 0.5 * swiglu(x, p['ffn2_up'], p['ffn2_down']), p['ln_g'][2], p['ln_b'][2])
    return x, k, v, ki, h_re, h_im


def setup_inputs(seed: int = 0) -> dict:
    key = jax.random.key(seed)
    ks = jax.random.split(key, 32)
    f32 = jnp.float32
    n_pages = PAST_LEN // PAGE_SIZE
    used = DEC_BATCH * n_pages
    n_pool = used + max(1, used // 4)
    beta = (8.0 * DEPTH) ** -0.25
    nrm = lambda k, s, sc: jax.random.normal(k, s, f32) * sc

    x_prompt = nrm(ks[0], (BATCH, SEQ, D_MODEL), 1.0)
    x_sample = nrm(ks[1], (DEC_BATCH, DEC_SEQ, D_MODEL), 1.0)
    cache_k = nrm(ks[2], (DEPTH, n_pool, PAGE_SIZE, N_HEADS, HEAD_DIM), 1.0)
    cache_v = nrm(ks[3], (DEPTH, n_pool, PAGE_SIZE, N_HEADS, HEAD_DIM), beta)
    cache_kidx = nrm(ks[4], (DEPTH, n_pool, PAGE_SIZE, IDX_DIM), 1.0)
    state_ssm_re = nrm(ks[5], (DEPTH, DEC_BATCH, SSM_GROUPS, SSM_STATE), 0.1)
    state_ssm_im = nrm(ks[6], (DEPTH, DEC_BATCH, SSM_GROUPS, SSM_STATE), 0.1)
    page_table = jax.random.permutation(ks[7], n_pool)[:used].reshape(DEC_BATCH, n_pages).astype(jnp.int32)

    col_scale = jnp.concatenate([jnp.ones((SSM_WIDTH + 2 * ATT_WIDTH,), f32),
                                 jnp.full((ATT_WIDTH,), beta, f32),
                                 jnp.ones((IDX_HEADS * IDX_DIM + IDX_DIM + IDX_HEADS,), f32)])
    w_in = nrm(ks[8], (DEPTH, D_MODEL, D_IN), D_MODEL ** -0.5) * col_scale
    n_idx = jnp.arange(SSM_STATE, dtype=f32)
    ssm_lambda_re = -0.5 + nrm(ks[9], (DEPTH, SSM_GROUPS, SSM_STATE), 0.01)
    ssm_lambda_im = math.pi * n_idx + nrm(ks[10], (DEPTH, SSM_GROUPS, SSM_STATE), 0.01)
    ssm_log_step = jax.random.uniform(ks[11], (DEPTH, SSM_GROUPS), f32, math.log(DT_MIN), math.log(DT_MAX))
    ssm_b_re = nrm(ks[12], (DEPTH, SSM_GROUPS, SSM_STATE, SSM_GROUP), (2.0 * SSM_GROUP) ** -0.5)
    ssm_b_im = nrm(ks[13], (DEPTH, SSM_GROUPS, SSM_STATE, SSM_GROUP), (2.0 * SSM_GROUP) ** -0.5)
    ssm_c_re = nrm(ks[14], (DEPTH, SSM_GROUPS, SSM_GROUP, SSM_STATE), SSM_STATE ** -0.5)
    ssm_c_im = nrm(ks[15], (DEPTH, SSM_GROUPS, SSM_GROUP, SSM_STATE), SSM_STATE ** -0.5)
    ssm_d = nrm(ks[16], (DEPTH, SSM_GROUPS, SSM_GROUP), 1.0)
    w_glu = nrm(ks[17], (DEPTH, SSM_WIDTH, SSM_WIDTH), SSM_WIDTH ** -0.5)
    b_glu = nrm(ks[18], (DEPTH, SSM_WIDTH), 0.02)
    w_out = nrm(ks[19], (DEPTH, D_MODEL, D_MODEL), beta * D_MODEL ** -0.5)
    ffn1_up = nrm(ks[20], (DEPTH, D_MODEL, 2 * D_FF), D_MODEL ** -0.5)
    ffn1_down = nrm(ks[21], (DEPTH, D_FF, D_MODEL), beta * D_FF ** -0.5)
    ffn2_up = nrm(ks[22], (DEPTH, D_MODEL, 2 * D_FF), D_MODEL ** -0.5)
    ffn2_down = nrm(ks[23], (DEPTH, D_FF, D_MODEL), beta * D_FF ** -0.5)
    ln_g = 1.0 + nrm(ks[24], (DEPTH, 3, D_MODEL), 0.02)
    ln_b = nrm(ks[25], (DEPTH, 3, D_MODEL), 0.02)
    return {"x_prompt": x_prompt, "x_sample": x_sample, "cache_k": cache_k, "cache_v": cache_v,
            "cache_kidx": cache_kidx, "state_ssm_re": state_ssm_re, "state_ssm_im": state_ssm_im,
            "page_table": page_table, "w_in": w_in, "ssm_lambda_re": ssm_lambda_re,
            "ssm_lambda_im": ssm_lambda_im, "ssm_log_step": ssm_log_step, "ssm_b_re": ssm_b_re,
            "ssm_b_im": ssm_b_im, "ssm_c_re": ssm_c_re, "ssm_c_im": ssm_c_im, "ssm_d": ssm_d,
            "w_glu": w_glu, "b_glu": b_glu, "w_out": w_out, "ffn1_up": ffn1_up, "ffn1_down": ffn1_down,
            "ffn2_up": ffn2_up, "ffn2_down": ffn2_down, "ln_g": ln_g, "ln_b": ln_b}


def reference(x_prompt, x_sample, cache_k, cache_v, cache_kidx, state_ssm_re, state_ssm_im, page_table,
              w_in, ssm_lambda_re, ssm_lambda_im, ssm_log_step, ssm_b_re, ssm_b_im, ssm_c_re, ssm_c_im,
              ssm_d, w_glu, b_glu, w_out, ffn1_up, ffn1_down, ffn2_up, ffn2_down, ln_g, ln_b):
    yp, ys = x_prompt, x_sample
    kp, vp, kip, hrp, hip = [], [], [], [], []
    kss, vss, kis, hrs, his = [], [], [], [], []
    zeros_state = jnp.zeros((x_prompt.shape[0], SSM_GROUPS, SSM_STATE), jnp.float32)
    for l in range(DEPTH):
        p = {'w_in': w_in[l], 'ssm_lambda_re': ssm_lambda_re[l], 'ssm_lambda_im': ssm_lambda_im[l],
             'ssm_log_step': ssm_log_step[l], 'ssm_b_re': ssm_b_re[l], 'ssm_b_im': ssm_b_im[l],
             'ssm_c_re': ssm_c_re[l], 'ssm_c_im': ssm_c_im[l], 'ssm_d': ssm_d[l], 'w_glu': w_glu[l],
             'b_glu': b_glu[l], 'w_out': w_out[l], 'ffn1_up': ffn1_up[l], 'ffn1_down': ffn1_down[l],
             'ffn2_up': ffn2_up[l], 'ffn2_down': ffn2_down[l], 'ln_g': ln_g[l], 'ln_b': ln_b[l]}
        yp, k1, v1, ki1, hr1, hi1 = decoder_layer(yp, zeros_state, zeros_state, p, prompt_sparse_attention)
        kp.append(k1); vp.append(v1); kip.append(ki1); hrp.append(hr1); hip.append(hi1)
        attn_s = functools.partial(sample_sparse_attention, cache_k_l=cache_k[l], cache_v_l=cache_v[l],
                                   cache_kidx_l=cache_kidx[l], page_table=page_table)
        ys, k2, v2, ki2, hr2, hi2 = decoder_layer(ys, state_ssm_re[l], state_ssm_im[l], p, attn_s)
        kss.append(k2); vss.append(v2); kis.append(ki2); hrs.append(hr2); his.append(hi2)
    return (yp, ys,
            jnp.stack(kp), jnp.stack(vp), jnp.stack(kip), jnp.stack(hrp), jnp.stack(hip),
            jnp.stack(kss), jnp.stack(vss), jnp.stack(kis), jnp.stack(hrs), jnp.stack(his))
```

```python
import math
from contextlib import ExitStack
import numpy as np
import concourse.bass as bass
import concourse.mybir as mybir
from concourse.bass_utils import run_bass_kernel_spmd

F32 = mybir.dt.float32
BF16 = mybir.dt.bfloat16
I32 = mybir.dt.int32
ALU = mybir.AluOpType
AF = mybir.ActivationFunctionType

L = 2
D = 1024
KT = 8
FT = 22
NP = 8192
NS = 128
NC = 256
NCH = NP // NC
ALPHA = (2.0 * L) ** 0.25
EPS2 = 1e-5 / (ALPHA * ALPHA)
NEG = -30000.0
NITER = 24
TWO_PI = 2.0 * math.pi
GK = math.sqrt(2.0 / math.pi)


class Buf:
    __slots__ = ("w", "r", "x")

    def __init__(self, x=False):
        self.w = None
        self.r = {}
        self.x = x


class Sched:
    def __init__(self, sems, dsems):
        self.E = ["pe", "act", "dve", "pool", "sp"]
        self.ops = {e: [] for e in self.E}
        self.cnt = {e: 0 for e in self.E}
        self.sem = sems
        self.dsems = dsems
        self.seen = {e: {} for e in self.E}
        self.duse = [0] * len(dsems)
        self.di = 0

    def _wait(self, eng, key, val):
        if key == ("e", "pe") and eng == "pe":
            return
        if self.seen[eng].get(key, 0) >= val:
            return
        self.seen[eng][key] = val
        sem = self.sem[key[1]] if key[0] == "e" else self.dsems[key[1]]
        self.ops[eng].append(lambda e, sem=sem, val=val: e.wait_ge(sem, val))

    def _deps(self, eng, reads, writes):
        for b in reads:
            if b.w:
                self._wait(eng, *b.w)
            if b.x:
                for k, v in b.r.items():
                    if k != ("e", eng):
                        self._wait(eng, k, v)
        for b in writes:
            if b.w:
                self._wait(eng, *b.w)
            for k, v in b.r.items():
                self._wait(eng, k, v)

    def _mark(self, tok, reads, writes):
        k, v = tok
        for b in writes:
            b.w = tok
            b.r = {}
        for b in reads:
            if b.r.get(k, 0) < v:
                b.r[k] = v

    def op(self, eng, fns, reads=(), writes=()):
        if not isinstance(fns, (list, tuple)):
            fns = [fns]
        self._deps(eng, reads, writes)
        self.cnt[eng] += 1
        tok = (("e", eng), self.cnt[eng])
        for f in fns[:-1]:
            self.ops[eng].append(f)
        self.ops[eng].append(lambda e, f=fns[-1], sem=self.sem[eng]: f(e).then_inc(sem, 1))
        self._mark(tok, reads, writes)

    def dma(self, q, fn, reads=(), writes=()):
        self._deps(q, reads, writes)
        slot = self.di % len(self.dsems)
        self.di += 1
        if self.duse[slot] > 0:
            self._wait(q, ("d", slot), 16 * self.duse[slot])
        self.duse[slot] += 1
        tok = (("d", slot), 16 * self.duse[slot])
        self.ops[q].append(lambda e, fn=fn, sem=self.dsems[slot]: fn(e).then_inc(sem, 16))
        self._mark(tok, reads, writes)

    def finish(self):
        for slot, u in enumerate(self.duse):
            if u > 0:
                self._wait("sp", ("d", slot), 16 * u)
        for e in ["pe", "act", "dve", "pool"]:
            if self.cnt[e] > 0:
                self._wait("sp", ("e", e), self.cnt[e])


CFG = {"layers": L, "nch": NCH, "sample": True, "rows": L * 2560 * 128, "dbg": False, "stop": 99, "sub": 99, "ntile": 99}


def build():
    nc = bass.Bass("TRN2", target_bir_lowering=False)

    def din(name, shape, d=F32):
        return nc.dram_tensor(name, list(shape), d, kind="ExternalInput").ap()

    def dout(name, shape):
        return nc.dram_tensor(name, list(shape), F32, kind="ExternalOutput").ap()

    def dscr(name, shape, d):
        return nc.dram_tensor(name, list(shape), d, kind="Internal").ap()

    xTp = din("xTp", [D, NP]); xTs = din("xTs", [D, NS])
    up = [din("up1", [L, FT, 128, 2048]), din("up2", [L, FT, 128, 2048])]
    dn = [din("dn1", [L, 16, 128, 1408]), din("dn2", [L, 16, 128, 1408])]
    winf = din("winf", [L, 10, 128, 2048])
    wtm = din("wtm", [L, 2, 128, 2080])
    wglu = din("wglu", [L, 128, 2048])
    wout = din("wout", [L, 4, 128, 2048])
    prm = din("prm", [L, 128, 56])
    sS = din("sS", [L, 128, 48])
    sB = din("sB", [L, 4, 128, 2560])
    sC = din("sC", [L, 128, 4096])
    h0 = din("h0", [L, 128, 512])
    ck = din("ck", [CFG["rows"], 512]); cv = din("cv", [CFG["rows"], 512]); cki = din("cki", [CFG["rows"], 32])
    ptb = din("ptb", [128, 256], I32)
    cst = din("cst", [128, 897])

    yTp = dout("yTp", [D, NP]); yTs = dout("yTs", [D, NS])
    kTp = dout("kTp", [L, 512, NP]); vp = dout("vp", [L, NP, 512]); kiTp = dout("kiTp", [L, 32, NP])
    hp = dout("hp", [L, 128, 32])
    kTs = dout("kTs", [L, 512, NS]); vs = dout("vs", [L, NS, 512]); kiTs = dout("kiTs", [L, 32, NS])
    hs = dout("hs", [L, 128, 512])
    x1p = dout("x1p", [D, NP]) if CFG["dbg"] else dscr("x1p", [D, NP], F32)
    x1s = dout("x1s", [D, NS]) if CFG["dbg"] else dscr("x1s", [D, NS], F32)
    kTh = dscr("kTh", [512, NP], BF16); kih = dscr("kih", [128, NP], BF16); Vh = dscr("Vh", [NP, 520], BF16)
    VSh = dscr("VSh", [NS, 520], BF16)

    es = ExitStack()
    with es:
        def sb(name, shape, d=F32):
            return es.enter_context(nc.sbuf_tensor(name, list(shape), d))

        sems = {e: es.enter_context(nc.semaphore("s_" + e)) for e in ["pe", "act", "dve", "pool"]}
        dsems = [es.enter_context(nc.semaphore("d%d" % i)) for i in range(24)]
        S = Sched(sems, dsems)

        ARENA = sb("arena", [128, 8192]); bARENA = Buf()
        HB = ARENA[:].bitcast(BF16)
        AI = ARENA[:].bitcast(I32)
        xT = sb("xT", [128, KT, NC]); bxT = Buf()
        xb = sb("xb", [128, KT, NC], BF16); bxb = Buf()
        wst = sb("wst", [128, 2080]); bwst = Buf()
        wbf = [sb("wbf%d" % i, [128, 2080], BF16) for i in range(2)]; bwbf = [Buf(), Buf()]
        CST = sb("CST", [128, 897]); bCST = Buf()
        identf = CST[:, 0:128]; tri = CST[:, 128:256]; taur = CST[:, 256:384]; taus = CST[:, 384:512]
        nfp = CST[:, 512:640]; nfs = CST[:, 640:768]; onesf = CST[:, 768:896]; iotaf = CST[:, 896:897]
        I4 = sb("I4", [128, 512], BF16); I8 = sb("I8", [8, 32], BF16); bI = Buf()
        PRM = sb("PRM", [128, 56]); bPRM = Buf()
        SS = sb("SS", [128, 48]); R16 = sb("R16", [128, 16]); ANG = sb("ANG", [128, 16]); STP = sb("STP", [128, 16]); bSS = Buf()
        cosT = sb("cosT", [128, 2048]); sinT = sb("sinT", [128, 2048]); rdT = sb("rdT", [128, 2048]); bTab = Buf()
        BBre = sb("BBre", [128, 2048], BF16); BBim = sb("BBim", [128, 2048], BF16)
        Cre = sb("Cre", [128, 2048], BF16); Cimn = sb("Cimn", [128, 2048], BF16)
        diagD = sb("diagD", [128, 512], BF16); bSSM = Buf()
        hpre = sb("hpre", [128, 16]); hpim = sb("hpim", [128, 16]); bhp = Buf()
        H0 = sb("H0", [128, 512]); RH0 = sb("RH0", [128, 512]); HSo = sb("HSo", [128, 512]); bH0 = Buf(); bHS = Buf()
        uT = sb("uT", [128, 4, NC], BF16); buT = Buf()
        qT = sb("qT", [128, 4, NC], BF16); bqT = Buf()
        kT = sb("kT", [128, 4, NC], BF16); bkT = Buf()
        kst = sb("kst", [128, NC]); bkst = Buf()
        wab = sb("wab", [128, 3, NC]); bwab = Buf()
        qip = sb("qip", [128, 3, NC], BF16); bqip = Buf()
        ki4 = sb("ki4", [128, NC], BF16); bki4 = Buf()
        kif = sb("kif", [32, NC]); bkif = Buf()
        vst = sb("vst", [128, 512]); bvst = Buf()
        Vb = sb("Vb", [128, NC // 128, 520], BF16); bVb = Buf()
        sgn = sb("sgn", [128, NC // 128, 8]); bsgn = Buf()
        sgnS = sb("sgnS", [8, 16, 8]); bsgnS = Buf()
        W = [sb("W%d" % i, [128, 512]) for i in range(6)]; bW = [Buf() for _ in range(6)]
        hbre = sb("hbre", [128, 512], BF16); hbim = sb("hbim", [128, 512], BF16); bhb = Buf()
        gT = sb("gT", [128, 4, NC], BF16); bgT = Buf()
        ysT = sb("ysT", [128, 4, NC], BF16); bysT = Buf()
        yattT = sb("yattT", [128, 4, NC], BF16); byattT = Buf()
        kit = [sb("kit%d" % i, [128, 512], BF16) for i in range(2)]; bkit = [Buf(), Buf()]
        kTt = [sb("kTt%d" % i, [128, 4, 128], BF16) for i in range(2)]; bkTt = [Buf(), Buf()]
        Vt = [sb("Vt%d" % i, [128, 520], BF16) for i in range(2)]; bVt = [Buf(), Buf()]
        Rh = [sb("Rh%d" % i, [128, 512], BF16) for i in range(3)]; bRh = [Buf() for _ in range(3)]
        sd = sb("sd", [128, 8, 128], BF16); bsd = Buf()
        mb = [sb("mb%d" % i, [128, 128], BF16) for i in range(2)]; bmb = [Buf(), Buf()]
        PT = [sb("PT%d" % i, [128, 1024], BF16) for i in range(2)]; bPT = [Buf(), Buf()]
        yat = sb("yat", [128, 512]); byat = Buf()
        rdn = sb("rdn", [128, 8]); brdn = Buf()
        lo = sb("lo", [128, 1]); mid = sb("mid", [128, 1]); cnt = sb("cnt", [128, 2]); tq = sb("tq", [128, 1]); bbis = Buf()
        junk = sb("junk", [128, 2048], BF16); bjunk = Buf()
        IDX = sb("IDX", [128, 256], I32); PTB = sb("PTB", [128, 256], I32); IDF = sb("IDF", [128, 256]); bIDX = Buf()
        kip = sb("kip", [128, 16, 32]); bkip = Buf()
        kitS = sb("kitS", [128, 2056], BF16); bkitS = Buf()
        kpg0 = sb("kpg0", [128, 512]); kpg = [kpg0, kpg0]; bkpg0 = Buf(); bkpg = [bkpg0, bkpg0]
        vpg0 = sb("vpg0", [128, 512]); vpg = [vpg0, vpg0]; bvpg0 = Buf(); bvpg = [bvpg0, bvpg0]
        PS = es.enter_context(nc.psum_tensor("PS", [128, 4096], F32))
        ps = [PS[:, i * 512:(i + 1) * 512] for i in range(8)]; bps = [Buf(True) for _ in range(8)]

        dcount = [0]

        def dq():
            dcount[0] += 1
            return "sp" if dcount[0] % 2 else "pool"

        def dma(out, in_, reads, writes, q=None):
            S.dma(q or "sp", lambda e: e.dma_start(out=out, in_=in_), reads, writes)

        def mm(out, lhsT, rhs, st, sp):
            return lambda e: e.matmul(out, lhsT, rhs, start=st, stop=sp, skip_group_check=True)

        def act(out, in_, func, reads, writes, bias=None, scale=None):
            kw = {}
            if bias is not None:
                kw["bias"] = bias
            if scale is not None:
                kw["scale"] = scale
            S.op("act", lambda e: e.activation(out=out, in_=in_, func=func, **kw), reads, writes)

        def ts(eng, out, in0, s1, s2, op0, op1, reads, writes, accum=None):
            kw = {}
            if op1 is not None:
                kw["op1"] = op1
            if accum is not None:
                kw["accum_out"] = accum
            S.op(eng, lambda e: e.tensor_scalar(out=out, in0=in0, scalar1=s1, scalar2=s2, op0=op0, **kw), reads, writes)

        def tt(eng, out, in0, in1, op, reads, writes):
            S.op(eng, lambda e: e.tensor_tensor(out=out, in0=in0, in1=in1, op=op), reads, writes)

        def stt(out, in0, scalar, in1, op0, op1, reads, writes):
            S.op("dve", lambda e: e.scalar_tensor_tensor(out=out, in0=in0, scalar=scalar, in1=in1, op0=op0, op1=op1), reads, writes)

        def cp(eng, out, in_, reads, writes):
            if eng == "act":
                S.op("act", lambda e: e.copy(out=out, in_=in_), reads, writes)
            else:
                S.op(eng, lambda e: e.tensor_copy(out=out, in_=in_), reads, writes)

        wslot = [0]

        def wload(src, n):
            i = wslot[0] % 2
            wslot[0] += 1
            dma(wst[:, :n], src, [], [bwst], q="sp")
            cp("pool", wbf[i][:, :n], wst[:, :n], [bwst], [bwbf[i]])
            return wbf[i], bwbf[i]

        dma(CST[:], cst[:, :], [], [bCST])
        dma(PTB[:], ptb[:, :], [], [bIDX])
        for r in range(4):
            cp("dve", I4[:, r * 128:(r + 1) * 128], identf, [bCST], [bI])
            cp("dve", I8[:, r * 8:(r + 1) * 8], CST[0:8, 0:8], [bCST], [bI])
        S.op("dve", lambda e: e.memset(Vb[:], 1.0), [], [bVb])
        S.op("dve", lambda e: e.memset(Vt[0][:], 1.0), [], [bVt[0]])
        S.op("dve", lambda e: e.memset(Vt[1][:], 1.0), [], [bVt[1]])
        cp("dve", IDF[:], PTB[:], [bIDX], [bIDX])

        def sincos(ang_ap, n, dsin, dcos, t0, t1, ti, rd, wr):
            for (dst, off) in ((dsin, 0.0), (dcos, 0.25)):
                ts("dve", t0, ang_ap, 1.0 / TWO_PI, off, ALU.mult, ALU.add, rd, wr)
                cp("dve", ti, t0, wr, wr)
                cp("dve", t1, ti, wr, wr)
                tt("dve", t0, t0, t1, ALU.subtract, wr, wr)
                stt(t1, t0, 0.5, t0, ALU.is_gt, ALU.subtract, wr, wr)
                act(dst, t1, AF.Sin, wr, wr, scale=-TWO_PI)

        def build_tables(tau_ap, nf_ap):
            A = lambda i: ARENA[:, i * 2048:(i + 1) * 2048]
            for s in range(16):
                ts("dve", ARENA[:, s * 128:(s + 1) * 128], tau_ap, ANG[:, s:s + 1], None, ALU.mult, None, [bSS, bCST], [bARENA])
                ts("dve", rdT[:, s * 128:(s + 1) * 128], nf_ap, R16[:, s:s + 1], None, ALU.mult, None, [bSS, bCST], [bTab])
            sincos(A(0), 2048, sinT[:], cosT[:], A(1), A(2), AI[:, 3 * 2048:4 * 2048], [bARENA], [bARENA, bTab])

        def layer_setup(l):
            dma(PRM[:], prm[l], [], [bPRM])
            dma(SS[:], sS[l], [], [bSS])
            dma(H0[:], h0[l], [], [bH0])
            act(STP[:], SS[:, 32:48], AF.Exp, [bSS], [bSS])
            tt("dve", R16[:], SS[:, 0:16], STP[:], ALU.mult, [bSS], [bSS])
            act(R16[:], R16[:], AF.Exp, [bSS], [bSS])
            tt("dve", ANG[:], SS[:, 16:32], STP[:], ALU.mult, [bSS], [bSS])
            A = lambda i: ARENA[:, i * 512:(i + 1) * 512]
            rw = [bARENA]
            for c in range(4):
                dma(ARENA[:, 0:2560], sB[l, c], [], rw)
                act(A(7), A(2), AF.Exp, rw, rw)
                tt("dve", A(8), A(0), A(7), ALU.mult, rw, rw)
                act(A(8), A(8), AF.Exp, rw, rw)
                tt("dve", A(9), A(1), A(7), ALU.mult, rw, rw)
                sincos(A(9), 512, A(10), A(11), A(12), A(13), AI[:, 14 * 512:15 * 512], rw, rw)
                tt("dve", A(11), A(11), A(8), ALU.mult, rw, rw)
                tt("dve", A(10), A(10), A(8), ALU.mult, rw, rw)
                ts("dve", A(11), A(11), -1.0, None, ALU.add, None, rw, rw)
                tt("dve", A(12), A(0), A(0), ALU.mult, rw, rw)
                tt("dve", A(13), A(1), A(1), ALU.mult, rw, rw)
                tt("dve", A(12), A(12), A(13), ALU.add, rw, rw)
                S.op("dve", lambda e: e.reciprocal(out=A(12), in_=A(12)), rw, rw)
                tt("dve", A(13), A(11), A(0), ALU.mult, rw, rw)
                tt("dve", A(14), A(10), A(1), ALU.mult, rw, rw)
                tt("dve", A(13), A(13), A(14), ALU.add, rw, rw)
                tt("dve", A(13), A(13), A(12), ALU.mult, rw, rw)
                tt("dve", A(14), A(10), A(0), ALU.mult, rw, rw)
                tt("dve", A(15), A(11), A(1), ALU.mult, rw, rw)
                tt("dve", A(14), A(14), A(15), ALU.subtract, rw, rw)
                tt("dve", A(14), A(14), A(12), ALU.mult, rw, rw)
                cs_ = slice(c * 512, (c + 1) * 512)
                tt("dve", A(7), A(13), A(3), ALU.mult, rw, rw)
                tt("dve", A(8), A(14), A(4), ALU.mult, rw, rw)
                tt("dve", BBre[:, cs_], A(7), A(8), ALU.subtract, rw, [bSSM])
                tt("dve", A(7), A(13), A(4), ALU.mult, rw, rw)
                tt("dve", A(8), A(14), A(3), ALU.mult, rw, rw)
                tt("dve", BBim[:, cs_], A(7), A(8), ALU.add, rw, [bSSM])
            dma(ARENA[:, 0:4096], sC[l], [], rw)
            cp("dve", Cre[:], ARENA[:, 0:2048], rw, [bSSM])
            ts("dve", Cimn[:], ARENA[:, 2048:4096], -1.0, None, ALU.mult, None, rw, [bSSM])
            for c in range(4):
                ts("dve", diagD[:, c * 128:(c + 1) * 128], identf, PRM[:, 52 + c:53 + c], None, ALU.mult, None, [bPRM, bCST], [bSSM])
            S.op("dve", lambda e: e.memset(hpre[:], 0.0), [], [bhp])
            S.op("dve", lambda e: e.memset(hpim[:], 0.0), [], [bhp])
            for s in range(16):
                ts("dve", RH0[:, s * 16:(s + 1) * 16], H0[:, s * 16:(s + 1) * 16], R16[:, s:s + 1], None, ALU.mult, None, [bH0, bSS], [bH0])
                ts("dve", RH0[:, 256 + s * 16:256 + (s + 1) * 16], H0[:, 256 + s * 16:256 + (s + 1) * 16], R16[:, s:s + 1], None, ALU.mult, None, [bH0, bSS], [bH0])
            build_tables(taur, nfp)

        def layernorm(l, idx, N):
            pm, pv = ps[6][:, :N], ps[7][:, :N]
            S.op("pe", [mm(pm, onesf, xT[:, kt, :N], kt == 0, kt == KT - 1) for kt in range(KT)], [bxT, bCST], [bps[6]])
            for kt in range(KT):
                tt("dve", xT[:, kt, :N], xT[:, kt, :N], pm, ALU.subtract, [bxT, bps[6]], [bxT])
            for kt in range(KT):
                w_ = W[kt % 2]
                act(w_[:, :N], xT[:, kt, :N], AF.Square, [bxT], [bW[kt % 2]])
                S.op("pe", [mm(pv, onesf, w_[:, :N], kt == 0, kt == KT - 1)], [bW[kt % 2], bCST], [bps[7]])
            ts("dve", W[2][:, :N], pv, EPS2, None, ALU.add, None, [bps[7]], [bW[2]])
            act(W[2][:, :N], W[2][:, :N], AF.Sqrt, [bW[2]], [bW[2]])
            S.op("dve", lambda e: e.reciprocal(out=W[2][:, :N], in_=W[2][:, :N]), [bW[2]], [bW[2]])
            for kt in range(KT):
                tt("dve", xT[:, kt, :N], xT[:, kt, :N], W[2][:, :N], ALU.mult, [bxT, bW[2]], [bxT])
                act(xT[:, kt, :N], xT[:, kt, :N], AF.Identity, [bxT, bPRM], [bxT],
                    bias=PRM[:, 24 + idx * 8 + kt:25 + idx * 8 + kt], scale=PRM[:, idx * 8 + kt:idx * 8 + kt + 1])
                cp("pool", xb[:, kt, :N], xT[:, kt, :N], [bxT], [bxb])

        def ffn(l, which, N):
            cff = 0.5 / ALPHA
            for j in range(FT):
                wb, bw = wload(up[which][l, j], 2048)
                pg, pu = ps[j % 2], ps[2 + j % 2]
                S.op("pe", [mm(pg[:, :N], wb[:, kt * 256:kt * 256 + 128], xb[:, kt, :N], kt == 0, kt == KT - 1) for kt in range(KT)],
                     [bw, bxb], [bps[j % 2]])
                S.op("pe", [mm(pu[:, :N], wb[:, kt * 256 + 128:kt * 256 + 256], xb[:, kt, :N], kt == 0, kt == KT - 1) for kt in range(KT)],
                     [bw, bxb], [bps[2 + j % 2]])
                w_ = W[j % 2]
                act(w_[:, :N], pg[:, :N], AF.Silu, [bps[j % 2]], [bW[j % 2]])
                tt("dve", HB[:, j * NC:j * NC + N], w_[:, :N], pu[:, :N], ALU.mult, [bW[j % 2], bps[2 + j % 2]], [bARENA])
            for i in range(KT):
                pd = ps[4 + i % 2]
                for hf in range(2):
                    wb, bw = wload(dn[which][l, i * 2 + hf], 1408)
                    S.op("pe", [mm(pd[:, :N], wb[:, k * 128:(k + 1) * 128], HB[:, (hf * 11 + k) * NC:(hf * 11 + k) * NC + N],
                                   hf == 0 and k == 0, hf == 1 and k == 10) for k in range(11)], [bw, bARENA], [bps[4 + i % 2]])
                stt(xT[:, i, :N], pd[:, :N], cff, xT[:, i, :N], ALU.mult, ALU.add, [bps[4 + i % 2], bxT], [bxT])

        def w_in(l, N, c0, sample):
            kTo, kio, vo = (kTs, kiTs, vs) if sample else (kTp, kiTp, vp)
            for pc in range(10):
                wb, bw = wload(winf[l, pc], 2048)
                for tt_ in range(2):
                    t = pc * 2 + tt_
                    if t > 18 or t >= CFG["ntile"]:
                        continue
                    p_, bp_ = ps[t % 4], bps[t % 4]
                    S.op("pe", [mm(p_[:, :N], wb[:, tt_ * 1024 + kt * 128:tt_ * 1024 + (kt + 1) * 128], xb[:, kt, :N], kt == 0, kt == KT - 1)
                                for kt in range(KT)], [bw, bxb], [bp_])
                    if t < 3:
                        cp("act", wab[:, t, :N], p_[:, :N], [bp_], [bwab])
                        stt(wab[:, t, :N], wab[:, t, :N], -1.0, wab[:, t, :N], ALU.mult, ALU.max, [bwab], [bwab])
                    elif t < 7:
                        cp("act", uT[:, t - 3, :N], p_[:, :N], [bp_], [buT])
                    elif t < 11:
                        cp("act", qT[:, t - 7, :N], p_[:, :N], [bp_], [bqT])
                    elif t < 15:
                        cp("act", kT[:, t - 11, :N], p_[:, :N], [bp_], [bkT])
                        ts("dve", kst[:, :N], p_[:, :N], 1.0, None, ALU.mult, None, [bp_], [bkst])
                        dma(kTo[l, (t - 11) * 128:(t - 10) * 128, c0:c0 + N], kst[:, :N], [bkst], [])
                    elif t < 18:
                        tt("dve", qip[:, t - 15, :N], p_[:, :N], wab[:, t - 15, :N], ALU.mult, [bp_, bwab], [bqip])
                    else:
                        cp("act", ki4[:, :N], p_[:, :N], [bp_], [bki4])
                        cp("dve", kif[:, :N], p_[0:32, :N], [bp_], [bkif])
                        dma(kio[l, :, c0:c0 + N], kif[0:32, :N], [bkif], [])
            if CFG["sub"] < 1:
                return
            wA, bA = wload(wtm[l, 0], 2080)
            wB, bB = wload(wtm[l, 1], 2080)
            nsub = 16 if sample else N // 128
            for sub in range(nsub):
                m = 8 if sample else 128
                cols = slice(sub * m, (sub + 1) * m)
                pV, pW = ps[4 + sub % 2], ps[6 + sub % 2]
                S.op("pe", [mm(pV[:m, :], xb[:, kt, cols], (wA if kt < 4 else wB)[:, (kt % 4) * 520:(kt % 4) * 520 + 512], kt == 0, kt == KT - 1)
                            for kt in range(KT)], [bA, bB, bxb], [bps[4 + sub % 2]])
                S.op("pe", [mm(pW[:m, 0:8], xb[:, kt, cols], (wA if kt < 4 else wB)[:, (kt % 4) * 520 + 512:(kt % 4) * 520 + 520], kt == 0, kt == KT - 1)
                            for kt in range(KT)], [bA, bB, bxb], [bps[6 + sub % 2]])
                cp("dve", vst[:m, :], pV[:m, :], [bps[4 + sub % 2]], [bvst])
                if sample:
                    dma(vo[l, sub * 8:(sub + 1) * 8, :], vst[:8, :], [bvst], [])
                    S.op("act", lambda e, pV=pV: e.copy(out=Vb[:8, 0, :].rearrange("p (h d) -> p h d", d=65)[:, :, 0:64],
                                                 in_=pV[:8, :].rearrange("p (h d) -> p h d", d=64)), [bps[4 + sub % 2]], [bVb])
                    dma(VSh[sub * 8:(sub + 1) * 8, :], Vb[:8, 0, :], [bVb], [bVSh])
                    act(sgnS[:, sub, :], pW[:8, 0:8], AF.Sign, [bps[6 + sub % 2]], [bsgnS])
                else:
                    dma(vo[l, c0 + sub * 128:c0 + (sub + 1) * 128, :], vst[:, :], [bvst], [])
                    S.op("act", lambda e, pV=pV, sub=sub: e.copy(out=Vb[:, sub, :].rearrange("p (h d) -> p h d", d=65)[:, :, 0:64],
                                                          in_=pV[:, :].rearrange("p (h d) -> p h d", d=64)), [bps[4 + sub % 2]], [bVb])
                    act(sgn[:, sub, :], pW[:, 0:8], AF.Sign, [bps[6 + sub % 2]], [bsgn])
            if CFG["sub"] < 2:
                return
            if not sample:
                dma(kTh[:, c0:c0 + N].rearrange("(i p) n -> p i n", p=128), kT[:, :, :N], [bkT], [bhist])
                dma(kih[:, c0:c0 + N], ki4[:, :N], [bki4], [bhist])
                dma(Vh[c0:c0 + N, :].rearrange("(s p) f -> p s f", p=128), Vb[:, 0:N // 128, :], [bVb], [bhist])

        bhist = Buf(); bVSh = Buf()

        def ssm(l, N, sample):
            nsc = N // 128
            for c in range(4):
                yp, byp = ps[4 + c % 2], bps[4 + c % 2]
                for sc in range(nsc):
                    t0 = sc * 128
                    bre, bim = ps[0 + (sc % 2) * 2], ps[1 + (sc % 2) * 2]
                    bbr, bbi = bps[0 + (sc % 2) * 2], bps[1 + (sc % 2) * 2]
                    S.op("pe", [mm(bre[:, j * 128:(j + 1) * 128], BBre[:, (c * 4 + j) * 128:(c * 4 + j + 1) * 128], uT[:, c, t0:t0 + 128], j == 0, j == 3)
                                for j in range(4)], [bSSM, buT], [bbr])
                    S.op("pe", [mm(bim[:, j * 128:(j + 1) * 128], BBim[:, (c * 4 + j) * 128:(c * 4 + j + 1) * 128], uT[:, c, t0:t0 + 128], j == 0, j == 3)
                                for j in range(4)], [bSSM, buT], [bbi])
                    cs = cosT[:, c * 512:(c + 1) * 512]; sn = sinT[:, c * 512:(c + 1) * 512]
                    tt("dve", W[2][:], bre, cs, ALU.mult, [bbr, bTab], [bW[2]])
                    tt("dve", W[3][:], bim, sn, ALU.mult, [bbi, bTab], [bW[3]])
                    tt("dve", W[0][:], W[2][:], W[3][:], ALU.add, [bW[2], bW[3]], [bW[0]])
                    tt("dve", W[2][:], bim, cs, ALU.mult, [bbi, bTab], [bW[2]])
                    tt("dve", W[3][:], bre, sn, ALU.mult, [bbr, bTab], [bW[3]])
                    tt("dve", W[1][:], W[2][:], W[3][:], ALU.subtract, [bW[2], bW[3]], [bW[1]])
                    if sample:
                        for (w_, bw_, off) in ((W[0], bW[0], 0), (W[1], bW[1], 256)):
                            v = w_[:, :].rearrange("p (j s t) -> p j s t", j=4, s=16)[:, :, :, 0]
                            r_ = RH0[:, off + c * 64:off + (c + 1) * 64].rearrange("p (j s) -> p j s", j=4)
                            tt("dve", v, v, r_, ALU.add, [bw_, bH0], [bw_])
                    for j in range(4):
                        sl = slice(j * 128, (j + 1) * 128)
                        s_ = c * 4 + j
                        ini_re = 0.0 if sample else hpre[:, s_:s_ + 1]
                        ini_im = 0.0 if sample else hpim[:, s_:s_ + 1]
                        S.op("dve", lambda e, sl=sl, s_=s_, ini=ini_re: e.tensor_tensor_scan(out=W[2][:, sl], data0=rdT[:, s_ * 128:(s_ + 1) * 128], data1=W[0][:, sl],
                                                                                       initial=ini, op0=ALU.mult, op1=ALU.add), [bW[0], bTab, bhp], [bW[2]])
                        S.op("dve", lambda e, sl=sl, s_=s_, ini=ini_im: e.tensor_tensor_scan(out=W[3][:, sl], data0=rdT[:, s_ * 128:(s_ + 1) * 128], data1=W[1][:, sl],
                                                                                       initial=ini, op0=ALU.mult, op1=ALU.add), [bW[1], bTab, bhp], [bW[3]])
                    tt("pool", W[4][:], W[2][:], cs, ALU.mult, [bW[2], bTab], [bW[4]])
                    tt("pool", W[5][:], W[3][:], sn, ALU.mult, [bW[3], bTab], [bW[5]])
                    tt("pool", W[0][:], W[4][:], W[5][:], ALU.subtract, [bW[4], bW[5]], [bW[0]])
                    tt("pool", W[4][:], W[3][:], cs, ALU.mult, [bW[3], bTab], [bW[4]])
                    tt("pool", W[5][:], W[2][:], sn, ALU.mult, [bW[2], bTab], [bW[5]])
                    tt("pool", W[1][:], W[4][:], W[5][:], ALU.add, [bW[4], bW[5]], [bW[1]])
                    if sample:
                        for (w_, bw_, off) in ((W[0], bW[0], 0), (W[1], bW[1], 256)):
                            v = w_[:, :].rearrange("p (j s t) -> p j s t", j=4, s=16)[:, :, :, 7]
                            o_ = HSo[:, off + c * 64:off + (c + 1) * 64].rearrange("p (j s) -> p j s", j=4)
                            cp("dve", o_, v, [bw_], [bHS])
                    else:
                        cp("dve", hpre[:, c * 4:c * 4 + 4], W[0][:, :].rearrange("p (j t) -> p j t", j=4)[:, :, 127], [bW[0]], [bhp])
                        cp("dve", hpim[:, c * 4:c * 4 + 4], W[1][:, :].rearrange("p (j t) -> p j t", j=4)[:, :, 127], [bW[1]], [bhp])
                    cp("act", hbre[:], W[0][:], [bW[0]], [bhb])
                    cp("act", hbim[:], W[1][:], [bW[1]], [bhb])
                    fns = []
                    for j in range(4):
                        s_ = c * 4 + j
                        fns.append(mm(yp[:, t0:t0 + 128], Cre[:, s_ * 128:(s_ + 1) * 128], hbre[:, j * 128:(j + 1) * 128], sc == 0 and j == 0, False))
                        fns.append(mm(yp[:, t0:t0 + 128], Cimn[:, s_ * 128:(s_ + 1) * 128], hbim[:, j * 128:(j + 1) * 128], False, False))
                    fns.append(mm(yp[:, t0:t0 + 128], diagD[:, c * 128:(c + 1) * 128], uT[:, c, t0:t0 + 128], False, sc == nsc - 1))
                    S.op("pe", fns, [bSSM, bhb, buT], [byp])
                act(W[4][:, :N], yp[:, :N], AF.Identity, [byp], [bW[4]], scale=0.5)
                act(W[5][:, :N], yp[:, :N], AF.Square, [byp], [bW[5]])
                ts("dve", W[5][:, :N], W[5][:, :N], 2.0 * GK * 0.044715, 2.0 * GK, ALU.mult, ALU.add, [bW[5]], [bW[5]])
                tt("dve", W[5][:, :N], W[5][:, :N], W[4][:, :N], ALU.mult, [bW[5], bW[4]], [bW[5]])
                act(W[5][:, :N], W[5][:, :N], AF.Tanh, [bW[5]], [bW[5]])
                stt(gT[:, c, :N], W[5][:, :N], 1.0, W[4][:, :N], ALU.add, ALU.mult, [bW[5], bW[4]], [bgT])
            wb, bw = wload(wglu[l], 2048)
            for m in range(4):
                p_, bp_ = ps[m % 2], bps[m % 2]
                S.op("pe", [mm(p_[:, :N], wb[:, (m * 4 + kt) * 128:(m * 4 + kt + 1) * 128], gT[:, kt, :N], kt == 0, kt == 3) for kt in range(4)], [bw, bgT], [bp_])
                act(W[m % 2][:, :N], p_[:, :N], AF.Sigmoid, [bp_, bPRM], [bW[m % 2]], bias=PRM[:, 48 + m:49 + m])
                tt("dve", ysT[:, m, :N], gT[:, m, :N], W[m % 2][:, :N], ALU.mult, [bgT, bW[m % 2]], [bysT])

        rot = {"k": 0, "v": 0, "kit": 0, "rh": 0, "mb": 0, "pt": 0, "x": 0}

        def qblock(nq, qsl, sgn_ap, bsg, stiles, kblocks, nk, tri_ap, dst):
            I_ = I4 if nq == 128 else I8
            for h in range(8):
                ts("pool", sd[:nq, h, :nq], identf[:nq, :nq], sgn_ap[:, h:h + 1], None, ALU.mult, None, [bCST, bsg], [bsd])
            off = 0
            for (w, get) in stiles:
                kt_, bk_ = get()
                sp_, bsp_ = ps[3], bps[3]
                for h in range(8):
                    xi = rot["x"] % 3; rot["x"] += 1
                    ri = rot["rh"] % 3; rot["rh"] += 1
                    j = h % 3
                    S.op("pe", [mm(ps[xi][:nq, :w], qip[32 * j:32 * j + 32, h // 3, qsl], kt_[32 * j:32 * j + 32, :w], True, True)], [bqip, bk_], [bps[xi]])
                    act(Rh[ri][:nq, :w], ps[xi][:nq, :w], AF.Relu, [bps[xi]], [bRh[ri]])
                    S.op("pe", [mm(sp_[:nq, :w], sd[:nq, h, :nq], Rh[ri][:nq, :w], h == 0, h == 7)], [bsd, bRh[ri]], [bsp_])
                cp("dve", ARENA[:nq, off:off + w], sp_[:nq, :w], [bsp_], [bARENA])
                off += w
            kwl = kblocks[-1][0]
            tt("dve", ARENA[:nq, nk - kwl:nk], ARENA[:nq, nk - kwl:nk], tri_ap, ALU.add, [bARENA, bCST], [bARENA])
            S.op("dve", lambda e: e.memset(lo[:], -2048.0), [], [bbis])
            if nk > 256:
                for it in range(NITER):
                    wk = 2048.0 / (2.0 ** it)
                    ts("dve", mid[:nq], lo[:nq], wk, None, ALU.add, None, [bbis], [bbis])
                    npc = (nk + 2047) // 2048
                    for pc in range(npc):
                        a, b = pc * 2048, min(nk, (pc + 1) * 2048)
                        s2 = None if pc == 0 else cnt[:nq, (pc - 1) % 2:(pc - 1) % 2 + 1]
                        ts("dve", junk[:nq, :b - a], ARENA[:nq, a:b], mid[:nq], s2, ALU.is_gt, ALU.add, [bARENA, bbis], [bjunk, bbis],
                           accum=cnt[:nq, pc % 2:pc % 2 + 1])
                    cl = cnt[:nq, (npc - 1) % 2:(npc - 1) % 2 + 1]
                    ts("dve", tq[:nq], cl, 255.5, wk, ALU.is_gt, ALU.mult, [bbis], [bbis])
                    tt("dve", lo[:nq], lo[:nq], tq[:nq], ALU.add, [bbis], [bbis])
            O = [ps[6], ps[7]]; bO = [bps[6], bps[7]]
            LT = PS[:, 4 * 512:6 * 512]; bLT = [bps[4], bps[5]]
            off = 0
            nb = len(kblocks)
            for bi, (kw, get) in enumerate(kblocks):
                kk, bkk, vv, bvv = get()
                mi = rot["mb"] % 2; rot["mb"] += 1
                pi = rot["pt"] % 2; rot["pt"] += 1
                ts("dve", mb[mi][:nq, :kw], ARENA[:nq, off:off + kw], lo[:nq], NEG, ALU.is_le, ALU.mult, [bARENA, bbis], [bmb[mi]])
                fns = []
                for hf in range(2):
                    fns.append(mm(LT[:kw, hf * 512:hf * 512 + 4 * nq], mb[mi][:nq, :kw], I_[:nq, :4 * nq], True, False))
                for hh in range(4):
                    for hf in range(2):
                        h = hh * 2 + hf
                        fns.append(mm(LT[:kw, hf * 512 + hh * nq:hf * 512 + (hh + 1) * nq], kk[64 * hf:64 * hf + 64, hh, :kw],
                                      qT[64 * hf:64 * hf + 64, hh, qsl], False, hh == 3))
                S.op("pe", fns, [bmb[mi], bI, bkk, bqT], bLT)
                for hf in range(2):
                    act(PT[pi][:kw, hf * 512:hf * 512 + 4 * nq], LT[:kw, hf * 512:hf * 512 + 4 * nq], AF.Exp, bLT, [bPT[pi]], scale=0.125)
                fns = []
                for h in range(8):
                    hf, hh = h % 2, h // 2
                    fns.append(mm(O[h // 4][:nq, (h % 4) * 65:(h % 4) * 65 + 65], PT[pi][:kw, hf * 512 + hh * nq:hf * 512 + (hh + 1) * nq],
                                  vv[:kw, h * 65:h * 65 + 65], bi == 0 and h % 4 == 0, bi == nb - 1 and h % 4 == 3))
                S.op("pe", fns, [bPT[pi], bvv], bO)
                off += kw
            for h in range(8):
                o_ = O[h // 4]
                S.op("dve", lambda e, o_=o_, h=h: e.reciprocal(out=rdn[:nq, h:h + 1], in_=o_[:nq, (h % 4) * 65 + 64:(h % 4) * 65 + 65]), bO, [brdn])
                ts("dve", yat[:nq, h * 64:(h + 1) * 64], o_[:nq, (h % 4) * 65:(h % 4) * 65 + 64], rdn[:nq, h:h + 1], None, ALU.mult, None, bO + [brdn], [byat])
            for i in range(4):
                xi = rot["x"] % 3; rot["x"] += 1
                S.op("pe", [lambda e, xi=xi, i=i: e.transpose(out=ps[xi][:, :nq], in_=yat[:nq, i * 128:(i + 1) * 128], identity=identf[:nq, :nq])], [byat, bCST], [bps[xi]])
                cp("act", dst(i), ps[xi][:, :nq], [bps[xi]], [byattT])

        def prompt_attention(N, c0):
            for sub in range(N // 128):
                qb = (c0 + sub * 128) // 128
                nk = 128 * (qb + 1)
                stiles = []
                for st in range((nk + 511) // 512):
                    w = min(512, nk - st * 512)

                    def get(st=st, w=w):
                        i = rot["kit"] % 2; rot["kit"] += 1
                        dma(kit[i][:, :w], kih[:, st * 512:st * 512 + w], [bhist], [bkit[i]], q=dq())
                        return kit[i], bkit[i]
                    stiles.append((w, get))
                kblocks = []
                for sb_ in range(qb + 1):
                    def getk(sb_=sb_):
                        i = rot["k"] % 2; rot["k"] += 1
                        dma(kTt[i][:, :, :], kTh[:, sb_ * 128:(sb_ + 1) * 128].rearrange("(i p) n -> p i n", p=128), [bhist], [bkTt[i]], q=dq())
                        dma(Vt[i][:, :], Vh[sb_ * 128:(sb_ + 1) * 128, :], [bhist], [bVt[i]], q=dq())
                        return kTt[i], bkTt[i], Vt[i], bVt[i]
                    kblocks.append((128, getk))
                qblock(128, slice(sub * 128, (sub + 1) * 128), sgn[:, sub, :], bsgn, stiles, kblocks, nk, tri,
                       lambda i, sub=sub: yattT[:, i, sub * 128:(sub + 1) * 128])

        def sample_attention(l):
            for seq in range(16):
                for pg in range(16):
                    S.dma("pool", lambda e, pg=pg, seq=seq: e.indirect_dma_start(
                        out=kip[:, pg, :], out_offset=None, in_=cki[:, :],
                        in_offset=bass.IndirectOffsetOnAxis(ap=IDX[:, seq * 16 + pg:seq * 16 + pg + 1], axis=0)), [bIDX], [bkip])
                for g in range(4):
                    xi = rot["x"] % 3; rot["x"] += 1
                    S.op("pe", [lambda e, xi=xi, g=g, k=k: e.transpose(out=ps[xi][0:32, k * 128:(k + 1) * 128], in_=kip[:, g * 4 + k, :], identity=identf)
                                for k in range(4)], [bkip, bCST], [bps[xi]])
                    for r in range(4):
                        cp("act" if r % 2 else "dve", kitS[32 * r:32 * r + 32, g * 512:(g + 1) * 512], ps[xi][0:32, :], [bps[xi]], [bkitS])
                cp("dve", kitS[:, 2048:2056], ki4[:, seq * 8:(seq + 1) * 8], [bki4], [bkitS])
                stiles = []
                for st in range(5):
                    w = 512 if st < 4 else 8
                    stiles.append((w, lambda st=st, w=w: (kitS[:, st * 512:st * 512 + w], bkitS)))
                kblocks = []
                for pg in range(16):
                    def getk(pg=pg, seq=seq):
                        i = rot["k"] % 2; rot["k"] += 1
                        S.dma("pool", lambda e: e.indirect_dma_start(out=kpg[i][:, :], out_offset=None, in_=ck[:, :],
                              in_offset=bass.IndirectOffsetOnAxis(ap=IDX[:, seq * 16 + pg:seq * 16 + pg + 1], axis=0)), [bIDX], [bkpg[i]])
                        S.dma("pool", lambda e: e.indirect_dma_start(out=vpg[i][:, :], out_offset=None, in_=cv[:, :],
                              in_offset=bass.IndirectOffsetOnAxis(ap=IDX[:, seq * 16 + pg:seq * 16 + pg + 1], axis=0)), [bIDX], [bvpg[i]])
                        xi = rot["x"] % 3; rot["x"] += 1
                        S.op("pe", [lambda e, k=k: e.transpose(out=ps[xi][:, k * 128:(k + 1) * 128], in_=kpg[i][:, k * 128:(k + 1) * 128], identity=identf)
                                    for k in range(4)], [bkpg[i], bCST], [bps[xi]])
                        cp("act", kTt[i][:, :, :].rearrange("p i n -> p (i n)"), ps[xi][:, :], [bps[xi]], [bkTt[i]])
                        S.op("pool", lambda e: e.tensor_copy(out=Vt[i][:, :].rearrange("p (h d) -> p h d", d=65)[:, :, 0:64],
                                                      in_=vpg[i][:, :].rearrange("p (h d) -> p h d", d=64)), [bvpg[i]], [bVt[i]])
                        return kTt[i], bkTt[i], Vt[i], bVt[i]
                    kblocks.append((128, getk))

                def getn(seq=seq):
                    i = rot["k"] % 2; rot["k"] += 1
                    cp("dve", kTt[i][:, :, 0:8], kT[:, :, seq * 8:(seq + 1) * 8], [bkT], [bkTt[i]])
                    dma(Vt[i][0:8, :], VSh[seq * 8:(seq + 1) * 8, :], [bVSh], [bVt[i]])
                    return kTt[i], bkTt[i], Vt[i], bVt[i]
                kblocks.append((8, getn))
                qblock(8, slice(seq * 8, (seq + 1) * 8), sgnS[:, seq, :], bsgnS, stiles, kblocks, 2056, CST[0:8, 128:136],
                       lambda i, seq=seq: yattT[:, i, seq * 8:(seq + 1) * 8])

        def w_out(l, N):
            for pc in range(4):
                wb, bw = wload(wout[l, pc], 2048)
                for mm_ in range(2):
                    i = pc * 2 + mm_
                    p_, bp_ = ps[i % 2], bps[i % 2]
                    S.op("pe", [mm(p_[:, :N], wb[:, mm_ * 1024 + kt * 128:mm_ * 1024 + (kt + 1) * 128], (ysT if kt < 4 else yattT)[:, kt % 4, :N], kt == 0, kt == KT - 1)
                                for kt in range(KT)], [bw, bysT, byattT], [bp_])
                    stt(xT[:, i, :N], p_[:, :N], 1.0 / ALPHA, xT[:, i, :N], ALU.mult, ALU.add, [bp_, bxT], [bxT])

        bx1 = [Buf() for _ in range(NCH + 1)]
        ST = CFG["stop"]
        for l in range(CFG["layers"]):
            if ST > 0:
                layer_setup(l)
            for ci in list(range(CFG["nch"])) + ([NCH] if CFG["sample"] else []):
                sample = ci == NCH
                N = NS if sample else NC
                c0 = 0 if sample else ci * NC
                if sample:
                    build_tables(taus, nfs)
                    cp("dve", IDF[:], PTB[:], [bIDX], [bIDX])
                    ts("dve", IDF[:], IDF[:], 128.0, float(l * 2560 * 128), ALU.mult, ALU.add, [bIDX], [bIDX])
                    ts("dve", IDF[:], IDF[:], iotaf, None, ALU.add, None, [bIDX, bCST], [bIDX])
                    cp("dve", IDX[:], IDF[:], [bIDX], [bIDX])
                src = (xTs if sample else xTp) if l == 0 else (x1s if sample else x1p)
                if ST > 1:
                    dma(xT[:, :, :N], src[:, c0:c0 + N].rearrange("(k p) n -> p k n", p=128), [bx1[ci]], [bxT])
                    for kt in range(KT):
                        cp("pool", xb[:, kt, :N], xT[:, kt, :N], [bxT], [bxb])
                if ST > 2:
                    ffn(l, 0, N)
                if ST > 3:
                    layernorm(l, 0, N)
                if ST > 4:
                    w_in(l, N, c0, sample)
                if ST > 5:
                    ssm(l, N, sample)
                if ST > 6:
                    if sample:
                        sample_attention(l)
                        dma(hs[l], HSo[:], [bHS], [])
                    else:
                        prompt_attention(N, c0)
                if ST > 7:
                    w_out(l, N)
                    layernorm(l, 1, N)
                    ffn(l, 1, N)
                    layernorm(l, 2, N)
                dst = (x1s if sample else x1p) if l == 0 else (yTs if sample else yTp)
                if ST > 1:
                    dma(dst[:, c0:c0 + N].rearrange("(k p) n -> p k n", p=128), xT[:, :, :N], [bxT], [bx1[ci]])
                if ci == CFG["nch"] - 1:
                    dma(hp[l, :, 0:16], hpre[:], [bhp], [])
                    dma(hp[l, :, 16:32], hpim[:], [bhp], [])
        S.finish()

        with nc.Block() as block:
            @block.tensor
            def _(e):
                for f in S.ops["pe"]:
                    f(e)

            @block.scalar
            def _(e):
                for f in S.ops["act"]:
                    f(e)

            @block.vector
            def _(e):
                for f in S.ops["dve"]:
                    f(e)

            @block.gpsimd
            def _(e):
                for f in S.ops["pool"]:
                    f(e)

            @block.sync
            def _(e):
                for f in S.ops["sp"]:
                    f(e)
    return nc, {e: len(v) for e, v in S.ops.items()}


def _prep(inp):
    f = np.float32
    g = {}
    def C(a):
        return np.ascontiguousarray(a, dtype=f)
    for w, name in ((0, "ffn1_up"), (1, "ffn2_up")):
        W = inp[name].reshape(L, KT, 128, 2, FT, 128)
        g["up%d" % (w + 1)] = C(W.transpose(0, 4, 2, 1, 3, 5).reshape(L, FT, 128, 2048))
    for w, name in ((0, "ffn1_down"), (1, "ffn2_down")):
        W = inp[name].reshape(L, 2, 11, 128, 8, 128)
        g["dn%d" % (w + 1)] = C(W.transpose(0, 4, 1, 3, 2, 5).reshape(L, 16, 128, 1408))
    win = inp["w_in"]
    u, q, k, v = win[:, :, 0:512], win[:, :, 512:1024], win[:, :, 1024:1536], win[:, :, 1536:2048]
    qi, ki, wi = win[:, :, 2048:2304], win[:, :, 2304:2336], win[:, :, 2336:2344]
    wexp = np.repeat(wi, 32, axis=2)
    ki4 = np.tile(ki, (1, 1, 4))
    qi3 = np.zeros((L, D, 384), f); we3 = np.zeros((L, D, 384), f)
    for h in range(8):
        o = (h // 3) * 128 + (h % 3) * 32
        qi3[:, :, o:o + 32] = qi[:, :, h * 32:(h + 1) * 32]
        we3[:, :, o:o + 32] = wexp[:, :, h * 32:(h + 1) * 32]
    fm = np.concatenate([we3, u, q, k, qi3, ki4, np.zeros((L, D, 128), f)], axis=2)
    fm = fm.reshape(L, KT, 128, 10, 2, 128)
    g["winf"] = C(fm.transpose(0, 3, 2, 4, 1, 5).reshape(L, 10, 128, 2048))
    tm = np.concatenate([v, wi], axis=2).reshape(L, 2, 4, 128, 520)
    g["wtm"] = C(tm.transpose(0, 1, 3, 2, 4).reshape(L, 2, 128, 2080))
    g["wglu"] = C(inp["w_glu"].reshape(L, 4, 128, 4, 128).transpose(0, 2, 3, 1, 4).reshape(L, 128, 2048))
    wo = inp["w_out"].reshape(L, KT, 128, 4, 2, 128)
    g["wout"] = C(wo.transpose(0, 3, 2, 4, 1, 5).reshape(L, 4, 128, 2048))
    prm = np.zeros((L, 128, 56), f)
    prm[:, :, 0:24] = inp["ln_g"].reshape(L, 3, KT, 128).transpose(0, 3, 1, 2).reshape(L, 128, 24)
    prm[:, :, 24:48] = inp["ln_b"].reshape(L, 3, KT, 128).transpose(0, 3, 1, 2).reshape(L, 128, 24)
    prm[:, :, 48:52] = inp["b_glu"].reshape(L, 4, 128).transpose(0, 2, 1)
    prm[:, :, 52:56] = inp["ssm_d"].reshape(L, 4, 128).transpose(0, 2, 1)
    g["prm"] = prm
    st = lambda a: a.reshape(L, 16, 128).transpose(0, 2, 1)
    lst = np.repeat(inp["ssm_log_step"][:, :, None], 64, axis=2)
    g["sS"] = C(np.concatenate([st(inp["ssm_lambda_re"]), st(inp["ssm_lambda_im"]), st(lst)], axis=2))
    def bbl(a):
        x = a.reshape(L, 4, 4, 128)
        x = x.transpose(0, 2, 1, 3)[:, :, None]
        return np.broadcast_to(x, (L, 4, 32, 4, 128)).reshape(L, 128, 512)
    def bexp(b):
        o = np.zeros((L, 4, 2, 16, 4, 2, 64), f)
        bb = b.reshape(L, 4, 4, 2, 64, 16)
        for g2 in range(2):
            o[:, :, g2, :, :, g2, :] = bb[:, :, :, g2].transpose(0, 2, 4, 1, 3)
        return o.reshape(L, 128, 512)
    def zl(a):
        x = a.reshape(L, 4, 1, 512)
        return np.broadcast_to(x, (L, 4, 128, 512))
    def bz(be):
        x = be.reshape(L, 4, 32, 4, 128)
        o = np.zeros((L, 4, 4, 32, 4, 128), f)
        for j in range(4):
            o[:, :, j, :, j, :] = x[:, j].transpose(0, 2, 1, 3)
        return o.reshape(L, 4, 128, 512)
    g["sB"] = C(np.concatenate([zl(inp["ssm_lambda_re"].reshape(L, 16, 128)), zl(inp["ssm_lambda_im"].reshape(L, 16, 128)),
                                zl(lst.reshape(L, 16, 128)), bz(bexp(inp["ssm_b_re"])), bz(bexp(inp["ssm_b_im"]))], axis=3))
    def cexp(cm):
        o = np.zeros((L, 2, 64, 16, 4, 2, 16), f)
        cc = cm.reshape(L, 16, 2, 16, 64)
        for s in range(16):
            for g2 in range(2):
                o[:, g2, :, s, s % 4, g2, :] = cc[:, s, g2].transpose(0, 2, 1)
        return o.reshape(L, 128, 2048)
    g["sC"] = C(np.concatenate([cexp(inp["ssm_c_re"]), cexp(inp["ssm_c_im"])], axis=2))
    ck = inp["cache_k"].reshape(L * 2560 * 128, 512)
    cv = inp["cache_v"].reshape(L * 2560 * 128, 512)
    cki = inp["cache_kidx"].reshape(L * 2560 * 128, 32)
    cst = np.zeros((128, 897), f)
    cst[:, 0:128] = np.eye(128)
    cst[:, 128:256] = np.where(np.arange(128)[None, :] > np.arange(128)[:, None], NEG, 0.0)
    cst[:, 256:384] = np.arange(1, 129)[None, :]
    cst[:, 384:512] = np.tile(np.arange(1, 9), 16)[None, :]
    cst[:, 512:640] = 1.0; cst[:, 512] = 0.0; cst[:, 512:640][:, 0] = 1.0
    nfs = np.ones(128, f); nfs[0::8] = 0.0
    cst[:, 640:768] = nfs[None, :]
    cst[:, 768:896] = 1.0 / 1024.0
    cst[:, 896] = np.arange(128)
    g["cst"] = cst
    cores = []
    for c in range(8):
        b = c % 2
        d = dict(g)
        d["xTp"] = C(inp["x_prompt"][b].T)
        d["xTs"] = C(inp["x_sample"][16 * c:16 * c + 16].reshape(128, D).T)
        h0 = np.zeros((L, 128, 512), f)
        for i, nm in enumerate(("state_ssm_re", "state_ssm_im")):
            a = inp[nm][:, 16 * c:16 * c + 16].reshape(L, 16, 16, 128)
            h0[:, :, i * 256:(i + 1) * 256] = a.transpose(0, 3, 2, 1).reshape(L, 128, 256)
        d["h0"] = h0
        d["ck"], d["cv"], d["cki"] = ck, cv, cki
        d["ptb"] = np.ascontiguousarray(np.broadcast_to(inp["page_table"][16 * c:16 * c + 16].reshape(1, 256), (128, 256)).astype(np.int32))
        cores.append(d)
    return cores


_NC = None


def kernel(**inputs):
    global _NC
    inp = {k: np.asarray(v) for k, v in inputs.items()}
    if _NC is None:
        _NC = build()[0]
    in_maps = _prep(inp)
    res = run_bass_kernel_spmd(_NC, in_maps, core_ids=list(range(8))).results
    f = np.float32
    B = 2
    y_p = np.stack([res[b]["yTp"].T for b in range(B)]).astype(f)
    y_s = np.concatenate([res[c]["yTs"].T.reshape(16, 8, D) for c in range(8)]).astype(f)
    k_p = np.stack([np.stack([res[b]["kTp"][l].T.reshape(NP, 8, 64) for b in range(B)]) for l in range(L)]).astype(f)
    v_p = np.stack([np.stack([res[b]["vp"][l].reshape(NP, 8, 64) for b in range(B)]) for l in range(L)]).astype(f)
    ki_p = np.stack([np.stack([res[b]["kiTp"][l].T for b in range(B)]) for l in range(L)]).astype(f)
    def hst(a):
        return a.T.reshape(32, 64)
    hr_p = np.stack([np.stack([hst(res[b]["hp"][l][:, 0:16]) for b in range(B)]) for l in range(L)]).astype(f)
    hi_p = np.stack([np.stack([hst(res[b]["hp"][l][:, 16:32]) for b in range(B)]) for l in range(L)]).astype(f)
    k_s = np.stack([np.concatenate([res[c]["kTs"][l].T.reshape(16, 8, 8, 64) for c in range(8)]) for l in range(L)]).astype(f)
    v_s = np.stack([np.concatenate([res[c]["vs"][l].reshape(16, 8, 8, 64) for c in range(8)]) for l in range(L)]).astype(f)
    ki_s = np.stack([np.concatenate([res[c]["kiTs"][l].T.reshape(16, 8, 32) for c in range(8)]) for l in range(L)]).astype(f)
    def hss(a):
        return a.reshape(128, 16, 16).transpose(2, 1, 0).reshape(16, 32, 64)
    hr_s = np.stack([np.concatenate([hss(res[c]["hs"][l][:, 0:256]) for c in range(8)]) for l in range(L)]).astype(f)
    hi_s = np.stack([np.concatenate([hss(res[c]["hs"][l][:, 256:512]) for c in range(8)]) for l in range(L)]).astype(f)
    return (y_p, y_s, k_p, v_p, ki_p, hr_p, hi_p, k_s, v_s, ki_s, hr_s, hi_s)
```

```python
import math
from contextlib import ExitStack
import numpy as np
import concourse.bass as bass
import concourse.mybir as mybir
from concourse.bass_utils import run_bass_kernel_spmd

F32 = mybir.dt.float32
BF16 = mybir.dt.bfloat16
I32 = mybir.dt.int32
ALU = mybir.AluOpType
AF = mybir.ActivationFunctionType

L = 2
D = 1024
KT = 8
FT = 22
NP = 8192
NS = 128
NC = 256
NCH = NP // NC
ALPHA = (2.0 * L) ** 0.25
EPS2 = 1e-5 / (ALPHA * ALPHA)
NEG = -30000.0
NITER = 24
TWO_PI = 2.0 * math.pi
GK = math.sqrt(2.0 / math.pi)


class Buf:
    __slots__ = ("w", "r", "x")

    def __init__(self, x=False):
        self.w = None
        self.r = {}
        self.x = x


class Sched:
    def __init__(self, sems, dsems):
        self.E = ["pe", "act", "dve", "pool", "sp"]
        self.ops = {e: [] for e in self.E}
        self.cnt = {e: 0 for e in self.E}
        self.sem = sems
        self.dsems = dsems
        self.seen = {e: {} for e in self.E}
        self.duse = [0] * len(dsems)
        self.di = 0

    def _wait(self, eng, key, val):
        if key == ("e", "pe") and eng == "pe":
            return
        if self.seen[eng].get(key, 0) >= val:
            return
        self.seen[eng][key] = val
        sem = self.sem[key[1]] if key[0] == "e" else self.dsems[key[1]]
        self.ops[eng].append(lambda e, sem=sem, val=val: e.wait_ge(sem, val))

    def _deps(self, eng, reads, writes):
        for b in reads:
            if b.w:
                self._wait(eng, *b.w)
            if b.x:
                for k, v in b.r.items():
                    if k != ("e", eng):
                        self._wait(eng, k, v)
        for b in writes:
            if b.w:
                self._wait(eng, *b.w)
            for k, v in b.r.items():
                self._wait(eng, k, v)

    def _mark(self, tok, reads, writes):
        k, v = tok
        for b in writes:
            b.w = tok
            b.r = {}
        for b in reads:
            if b.r.get(k, 0) < v:
                b.r[k] = v

    def op(self, eng, fns, reads=(), writes=()):
        if not isinstance(fns, (list, tuple)):
            fns = [fns]
        self._deps(eng, reads, writes)
        self.cnt[eng] += 1
        tok = (("e", eng), self.cnt[eng])
        for f in fns[:-1]:
            self.ops[eng].append(f)
        self.ops[eng].append(lambda e, f=fns[-1], sem=self.sem[eng]: f(e).then_inc(sem, 1))
        self._mark(tok, reads, writes)

    def dma(self, q, fn, reads=(), writes=()):
        self._deps(q, reads, writes)
        slot = self.di % len(self.dsems)
        self.di += 1
        if self.duse[slot] > 0:
            self._wait(q, ("d", slot), 16 * self.duse[slot])
        self.duse[slot] += 1
        tok = (("d", slot), 16 * self.duse[slot])
        self.ops[q].append(lambda e, fn=fn, sem=self.dsems[slot]: fn(e).then_inc(sem, 16))
        self._mark(tok, reads, writes)

    def finish(self):
        for slot, u in enumerate(self.duse):
            if u > 0:
                self._wait("sp", ("d", slot), 16 * u)
        for e in ["pe", "act", "dve", "pool"]:
            if self.cnt[e] > 0:
                self._wait("sp", ("e", e), self.cnt[e])


CFG = {"layers": L, "nch": NCH, "sample": True, "rows": L * 2560 * 128, "dbg": False, "stop": 99, "sub": 99, "ntile": 99}


def build():
    nc = bass.Bass("TRN2", target_bir_lowering=False)

    def din(name, shape, d=F32):
        return nc.dram_tensor(name, list(shape), d, kind="ExternalInput").ap()

    def dout(name, shape):
        return nc.dram_tensor(name, list(shape), F32, kind="ExternalOutput").ap()

    def dscr(name, shape, d):
        return nc.dram_tensor(name, list(shape), d, kind="Internal").ap()

    xTp = din("xTp", [D, NP]); xTs = din("xTs", [D, NS])
    up = [din("up1", [L, FT, 128, 2048]), din("up2", [L, FT, 128, 2048])]
    dn = [din("dn1", [L, 16, 128, 1408]), din("dn2", [L, 16, 128, 1408])]
    winf = din("winf", [L, 10, 128, 2048])
    wtm = din("wtm", [L, 2, 128, 2080])
    wglu = din("wglu", [L, 128, 2048])
    wout = din("wout", [L, 4, 128, 2048])
    prm = din("prm", [L, 128, 56])
    sS = din("sS", [L, 128, 48])
    sB = din("sB", [L, 4, 128, 2560])
    sC = din("sC", [L, 128, 4096])
    h0 = din("h0", [L, 128, 512])
    ck = din("ck", [CFG["rows"], 512]); cv = din("cv", [CFG["rows"], 512]); cki = din("cki", [CFG["rows"], 32])
    ptb = din("ptb", [128, 256], I32)
    cst = din("cst", [128, 897])

    yTp = dout("yTp", [D, NP]); yTs = dout("yTs", [D, NS])
    kTp = dout("kTp", [L, 512, NP]); vp = dout("vp", [L, NP, 512]); kiTp = dout("kiTp", [L, 32, NP])
    hp = dout("hp", [L, 128, 32])
    kTs = dout("kTs", [L, 512, NS]); vs = dout("vs", [L, NS, 512]); kiTs = dout("kiTs", [L, 32, NS])
    hs = dout("hs", [L, 128, 512])
    x1p = dout("x1p", [D, NP]) if CFG["dbg"] else dscr("x1p", [D, NP], F32)
    x1s = dout("x1s", [D, NS]) if CFG["dbg"] else dscr("x1s", [D, NS], F32)
    kTh = dscr("kTh", [512, NP], BF16); kih = dscr("kih", [128, NP], BF16); Vh = dscr("Vh", [NP, 520], BF16)
    VSh = dscr("VSh", [NS, 520], BF16)
    wsc = {"up0": dscr("wup0", [L, FT, 128, 2048], BF16), "up1": dscr("wup1", [L, FT, 128, 2048], BF16),
           "dn0": dscr("wdn0", [L, 16, 128, 1408], BF16), "dn1": dscr("wdn1", [L, 16, 128, 1408], BF16),
           "winf": dscr("wwinf", [L, 10, 128, 2048], BF16), "wtm": dscr("wwtm", [L, 2, 128, 2080], BF16),
           "wglu": dscr("wwglu", [L, 1, 128, 2048], BF16), "wout": dscr("wwout", [L, 4, 128, 2048], BF16)}
    wsrc = {"up0": up[0], "up1": up[1], "dn0": dn[0], "dn1": dn[1], "winf": winf, "wtm": wtm,
            "wglu": wglu.rearrange("l (o p) n -> l o p n", o=1), "wout": wout}
    wnp = {"up0": (FT, 2048), "up1": (FT, 2048), "dn0": (16, 1408), "dn1": (16, 1408), "winf": (10, 2048), "wtm": (2, 2080), "wglu": (1, 2048), "wout": (4, 2048)}

    es = ExitStack()
    with es:
        def sb(name, shape, d=F32):
            return es.enter_context(nc.sbuf_tensor(name, list(shape), d))

        sems = {e: es.enter_context(nc.semaphore("s_" + e)) for e in ["pe", "act", "dve", "pool"]}
        dsems = [es.enter_context(nc.semaphore("d%d" % i)) for i in range(24)]
        S = Sched(sems, dsems)

        ARENA = sb("arena", [128, 8192]); bARENA = Buf()
        HB = ARENA[:].bitcast(BF16)
        AI = ARENA[:].bitcast(I32)
        xT = sb("xT", [128, KT, NC]); bxT = Buf()
        xb = sb("xb", [128, KT, NC], BF16); bxb = Buf()
        wst = sb("wst", [128, 2080]); bwst = Buf()
        NWB = 3
        wbf = [sb("wbf%d" % i, [128, 2080], BF16) for i in range(NWB)]; bwbf = [Buf() for _ in range(NWB)]
        CST = sb("CST", [128, 897]); bCST = Buf()
        identf = CST[:, 0:128]; tri = CST[:, 128:256]; taur = CST[:, 256:384]; taus = CST[:, 384:512]
        nfp = CST[:, 512:640]; nfs = CST[:, 640:768]; onesf = CST[:, 768:896]; iotaf = CST[:, 896:897]
        I4 = sb("I4", [128, 512], BF16); I8 = sb("I8", [8, 32], BF16); bI = Buf()
        PRM = sb("PRM", [128, 56]); bPRM = Buf()
        SS = sb("SS", [128, 48]); R16 = sb("R16", [128, 16]); ANG = sb("ANG", [128, 16]); STP = sb("STP", [128, 16]); bSS = Buf()
        cosT = sb("cosT", [128, 2048]); sinT = sb("sinT", [128, 2048]); rdT = sb("rdT", [128, 2048]); bTab = Buf()
        BBre = sb("BBre", [128, 2048], BF16); BBim = sb("BBim", [128, 2048], BF16)
        Cre = sb("Cre", [128, 2048], BF16); Cimn = sb("Cimn", [128, 2048], BF16)
        diagD = sb("diagD", [128, 512], BF16); bSSM = Buf()
        hpre = sb("hpre", [128, 16]); hpim = sb("hpim", [128, 16]); bhp = Buf()
        H0 = sb("H0", [128, 512]); RH0 = sb("RH0", [128, 512]); HSo = sb("HSo", [128, 512]); bH0 = Buf(); bHS = Buf()
        uT = sb("uT", [128, 4, NC], BF16); buT = Buf()
        qT = sb("qT", [128, 4, NC], BF16); bqT = Buf()
        kT = sb("kT", [128, 4, NC], BF16); bkT = Buf()
        kst = sb("kst", [128, NC]); bkst = Buf()
        wab = sb("wab", [128, 3, NC]); bwab = Buf()
        qip = sb("qip", [128, 3, NC], BF16); bqip = Buf()
        ki4 = sb("ki4", [128, NC], BF16); bki4 = Buf()
        kif = sb("kif", [32, NC]); bkif = Buf()
        vst = sb("vst", [128, 512]); bvst = Buf()
        Vb = sb("Vb", [128, NC // 128, 520], BF16); bVb = Buf()
        sgn = sb("sgn", [128, NC // 128, 8]); bsgn = Buf()
        sgnS = sb("sgnS", [8, 16, 8]); bsgnS = Buf()
        W = [sb("W%d" % i, [128, 512]) for i in range(6)]; bW = [Buf() for _ in range(6)]
        hbre = sb("hbre", [128, 512], BF16); hbim = sb("hbim", [128, 512], BF16); bhb = Buf()
        gT = sb("gT", [128, 4, NC], BF16); bgT = Buf()
        ysT = sb("ysT", [128, 4, NC], BF16); bysT = Buf()
        yattT = sb("yattT", [128, 4, NC], BF16); byattT = Buf()
        kit = [sb("kit%d" % i, [128, 512], BF16) for i in range(2)]; bkit = [Buf(), Buf()]
        kTt = [sb("kTt%d" % i, [128, 4, 128], BF16) for i in range(2)]; bkTt = [Buf(), Buf()]
        Vt = [sb("Vt%d" % i, [128, 520], BF16) for i in range(2)]; bVt = [Buf(), Buf()]
        Rh = [sb("Rh%d" % i, [128, 512], BF16) for i in range(3)]; bRh = [Buf() for _ in range(3)]
        sd = sb("sd", [128, 8, 128], BF16); bsd = Buf()
        mb = [sb("mb%d" % i, [128, 128], BF16) for i in range(2)]; bmb = [Buf(), Buf()]
        PT = [sb("PT%d" % i, [128, 1024], BF16) for i in range(2)]; bPT = [Buf(), Buf()]
        yat = sb("yat", [128, 512]); byat = Buf()
        rdn = sb("rdn", [128, 8]); brdn = Buf()
        lo = sb("lo", [128, 1]); mid = sb("mid", [128, 1]); cnt = sb("cnt", [128, 2]); tq = sb("tq", [128, 1]); bbis = Buf()
        junk = sb("junk", [128, 2048], BF16); bjunk = Buf()
        IDX = sb("IDX", [128, 256], I32); PTB = sb("PTB", [128, 256], I32); IDF = sb("IDF", [128, 256]); bIDX = Buf()
        kip = sb("kip", [128, 16, 32]); bkip = Buf()
        kitS = sb("kitS", [128, 2056], BF16); bkitS = Buf()
        kpg0 = sb("kpg0", [128, 512]); kpg = [kpg0, kpg0]; bkpg0 = Buf(); bkpg = [bkpg0, bkpg0]
        vpg0 = sb("vpg0", [128, 512]); vpg = [vpg0, vpg0]; bvpg0 = Buf(); bvpg = [bvpg0, bvpg0]
        PS = es.enter_context(nc.psum_tensor("PS", [128, 4096], F32))
        ps = [PS[:, i * 512:(i + 1) * 512] for i in range(8)]; bps = [Buf(True) for _ in range(8)]

        dcount = [0]

        def dq():
            dcount[0] += 1
            return "sp" if dcount[0] % 2 else "pool"

        def dma(out, in_, reads, writes, q=None):
            S.dma(q or "sp", lambda e: e.dma_start(out=out, in_=in_), reads, writes)

        def mm(out, lhsT, rhs, st, sp):
            return lambda e: e.matmul(out, lhsT, rhs, start=st, stop=sp, skip_group_check=True)

        def act(out, in_, func, reads, writes, bias=None, scale=None):
            kw = {}
            if bias is not None:
                kw["bias"] = bias
            if scale is not None:
                kw["scale"] = scale
            S.op("act", lambda e: e.activation(out=out, in_=in_, func=func, **kw), reads, writes)

        def ts(eng, out, in0, s1, s2, op0, op1, reads, writes, accum=None):
            kw = {}
            if op1 is not None:
                kw["op1"] = op1
            if accum is not None:
                kw["accum_out"] = accum
            S.op(eng, lambda e: e.tensor_scalar(out=out, in0=in0, scalar1=s1, scalar2=s2, op0=op0, **kw), reads, writes)

        def tt(eng, out, in0, in1, op, reads, writes):
            S.op(eng, lambda e: e.tensor_tensor(out=out, in0=in0, in1=in1, op=op), reads, writes)

        def stt(out, in0, scalar, in1, op0, op1, reads, writes):
            S.op("dve", lambda e: e.scalar_tensor_tensor(out=out, in0=in0, scalar=scalar, in1=in1, op0=op0, op1=op1), reads, writes)

        def cp(eng, out, in_, reads, writes):
            if eng == "act":
                S.op("act", lambda e: e.copy(out=out, in_=in_), reads, writes)
            else:
                S.op(eng, lambda e: e.tensor_copy(out=out, in_=in_), reads, writes)

        wslot = [0]
        bwsc = {}

        def wload(key, l, pc, n=None):
            n = wnp[key][1]
            i = wslot[0] % NWB
            wslot[0] += 1
            dma(wbf[i][:, :n], wsc[key][l, pc], [bwsc[(key, l, pc)]], [bwbf[i]], q=dq())
            return wbf[i], bwbf[i]

        def precast():
            for l in range(CFG["layers"]):
                for key in ["up0", "dn0", "winf", "wtm", "wglu", "wout", "up1", "dn1"]:
                    npc, n = wnp[key]
                    for pc in range(npc):
                        i = wslot[0] % NWB
                        wslot[0] += 1
                        b = Buf(); bwsc[(key, l, pc)] = b
                        dma(wst[:, :n], wsrc[key][l, pc], [], [bwst], q="sp")
                        cp("pool" if pc % 2 else "act", wbf[i][:, :n], wst[:, :n], [bwst], [bwbf[i]])
                        dma(wsc[key][l, pc], wbf[i][:, :n], [bwbf[i]], [b], q="sp")

        dma(CST[:], cst[:, :], [], [bCST])
        dma(PTB[:], ptb[:, :], [], [bIDX])
        for r in range(4):
            cp("dve", I4[:, r * 128:(r + 1) * 128], identf, [bCST], [bI])
            cp("dve", I8[:, r * 8:(r + 1) * 8], CST[0:8, 0:8], [bCST], [bI])
        S.op("dve", lambda e: e.memset(Vb[:], 1.0), [], [bVb])
        S.op("dve", lambda e: e.memset(Vt[0][:], 1.0), [], [bVt[0]])
        S.op("dve", lambda e: e.memset(Vt[1][:], 1.0), [], [bVt[1]])
        cp("dve", IDF[:], PTB[:], [bIDX], [bIDX])

        def sincos(ang_ap, n, dsin, dcos, t0, t1, ti, rd, wr):
            for (dst, off) in ((dsin, 0.0), (dcos, 0.25)):
                ts("dve", t0, ang_ap, 1.0 / TWO_PI, off, ALU.mult, ALU.add, rd, wr)
                cp("dve", ti, t0, wr, wr)
                cp("dve", t1, ti, wr, wr)
                tt("dve", t0, t0, t1, ALU.subtract, wr, wr)
                stt(t1, t0, 0.5, t0, ALU.is_gt, ALU.subtract, wr, wr)
                act(dst, t1, AF.Sin, wr, wr, scale=-TWO_PI)

        def build_tables(tau_ap, nf_ap):
            A = lambda i: ARENA[:, i * 2048:(i + 1) * 2048]
            for s in range(16):
                ts("dve", ARENA[:, s * 128:(s + 1) * 128], tau_ap, ANG[:, s:s + 1], None, ALU.mult, None, [bSS, bCST], [bARENA])
                ts("dve", rdT[:, s * 128:(s + 1) * 128], nf_ap, R16[:, s:s + 1], None, ALU.mult, None, [bSS, bCST], [bTab])
            sincos(A(0), 2048, sinT[:], cosT[:], A(1), A(2), AI[:, 3 * 2048:4 * 2048], [bARENA], [bARENA, bTab])

        def layer_setup(l):
            dma(PRM[:], prm[l], [], [bPRM])
            dma(SS[:], sS[l], [], [bSS])
            dma(H0[:], h0[l], [], [bH0])
            act(STP[:], SS[:, 32:48], AF.Exp, [bSS], [bSS])
            tt("dve", R16[:], SS[:, 0:16], STP[:], ALU.mult, [bSS], [bSS])
            act(R16[:], R16[:], AF.Exp, [bSS], [bSS])
            tt("dve", ANG[:], SS[:, 16:32], STP[:], ALU.mult, [bSS], [bSS])
            A = lambda i: ARENA[:, i * 512:(i + 1) * 512]
            rw = [bARENA]
            for c in range(4):
                dma(ARENA[:, 0:2560], sB[l, c], [], rw)
                act(A(7), A(2), AF.Exp, rw, rw)
                tt("dve", A(8), A(0), A(7), ALU.mult, rw, rw)
                act(A(8), A(8), AF.Exp, rw, rw)
                tt("dve", A(9), A(1), A(7), ALU.mult, rw, rw)
                sincos(A(9), 512, A(10), A(11), A(12), A(13), AI[:, 14 * 512:15 * 512], rw, rw)
                tt("dve", A(11), A(11), A(8), ALU.mult, rw, rw)
                tt("dve", A(10), A(10), A(8), ALU.mult, rw, rw)
                ts("dve", A(11), A(11), -1.0, None, ALU.add, None, rw, rw)
                tt("dve", A(12), A(0), A(0), ALU.mult, rw, rw)
                tt("dve", A(13), A(1), A(1), ALU.mult, rw, rw)
                tt("dve", A(12), A(12), A(13), ALU.add, rw, rw)
                S.op("dve", lambda e: e.reciprocal(out=A(12), in_=A(12)), rw, rw)
                tt("dve", A(13), A(11), A(0), ALU.mult, rw, rw)
                tt("dve", A(14), A(10), A(1), ALU.mult, rw, rw)
                tt("dve", A(13), A(13), A(14), ALU.add, rw, rw)
                tt("dve", A(13), A(13), A(12), ALU.mult, rw, rw)
                tt("dve", A(14), A(10), A(0), ALU.mult, rw, rw)
                tt("dve", A(15), A(11), A(1), ALU.mult, rw, rw)
                tt("dve", A(14), A(14), A(15), ALU.subtract, rw, rw)
                tt("dve", A(14), A(14), A(12), ALU.mult, rw, rw)
                cs_ = slice(c * 512, (c + 1) * 512)
                tt("dve", A(7), A(13), A(3), ALU.mult, rw, rw)
                tt("dve", A(8), A(14), A(4), ALU.mult, rw, rw)
                tt("dve", BBre[:, cs_], A(7), A(8), ALU.subtract, rw, [bSSM])
                tt("dve", A(7), A(13), A(4), ALU.mult, rw, rw)
                tt("dve", A(8), A(14), A(3), ALU.mult, rw, rw)
                tt("dve", BBim[:, cs_], A(7), A(8), ALU.add, rw, [bSSM])
            dma(ARENA[:, 0:4096], sC[l], [], rw)
            cp("dve", Cre[:], ARENA[:, 0:2048], rw, [bSSM])
            ts("dve", Cimn[:], ARENA[:, 2048:4096], -1.0, None, ALU.mult, None, rw, [bSSM])
            for c in range(4):
                ts("dve", diagD[:, c * 128:(c + 1) * 128], identf, PRM[:, 52 + c:53 + c], None, ALU.mult, None, [bPRM, bCST], [bSSM])
            S.op("dve", lambda e: e.memset(hpre[:], 0.0), [], [bhp])
            S.op("dve", lambda e: e.memset(hpim[:], 0.0), [], [bhp])
            for s in range(16):
                ts("dve", RH0[:, s * 16:(s + 1) * 16], H0[:, s * 16:(s + 1) * 16], R16[:, s:s + 1], None, ALU.mult, None, [bH0, bSS], [bH0])
                ts("dve", RH0[:, 256 + s * 16:256 + (s + 1) * 16], H0[:, 256 + s * 16:256 + (s + 1) * 16], R16[:, s:s + 1], None, ALU.mult, None, [bH0, bSS], [bH0])
            build_tables(taur, nfp)

        def layernorm(l, idx, N):
            pm, pv = ps[6][:, :N], ps[7][:, :N]
            S.op("pe", [mm(pm, onesf, xT[:, kt, :N], kt == 0, kt == KT - 1) for kt in range(KT)], [bxT, bCST], [bps[6]])
            for kt in range(KT):
                tt("dve", xT[:, kt, :N], xT[:, kt, :N], pm, ALU.subtract, [bxT, bps[6]], [bxT])
            for kt in range(KT):
                w_ = W[kt % 2]
                act(w_[:, :N], xT[:, kt, :N], AF.Square, [bxT], [bW[kt % 2]])
                S.op("pe", [mm(pv, onesf, w_[:, :N], kt == 0, kt == KT - 1)], [bW[kt % 2], bCST], [bps[7]])
            ts("dve", W[2][:, :N], pv, EPS2, None, ALU.add, None, [bps[7]], [bW[2]])
            act(W[2][:, :N], W[2][:, :N], AF.Sqrt, [bW[2]], [bW[2]])
            S.op("dve", lambda e: e.reciprocal(out=W[2][:, :N], in_=W[2][:, :N]), [bW[2]], [bW[2]])
            for kt in range(KT):
                tt("dve", xT[:, kt, :N], xT[:, kt, :N], W[2][:, :N], ALU.mult, [bxT, bW[2]], [bxT])
                act(xT[:, kt, :N], xT[:, kt, :N], AF.Identity, [bxT, bPRM], [bxT],
                    bias=PRM[:, 24 + idx * 8 + kt:25 + idx * 8 + kt], scale=PRM[:, idx * 8 + kt:idx * 8 + kt + 1])
                cp("pool", xb[:, kt, :N], xT[:, kt, :N], [bxT], [bxb])

        def ffn(l, which, N):
            cff = 0.5 / ALPHA
            for j in range(FT):
                wb, bw = wload("up%d" % which, l, j)
                pg, pu = ps[j % 2], ps[2 + j % 2]
                S.op("pe", [mm(pg[:, :N], wb[:, kt * 256:kt * 256 + 128], xb[:, kt, :N], kt == 0, kt == KT - 1) for kt in range(KT)],
                     [bw, bxb], [bps[j % 2]])
                S.op("pe", [mm(pu[:, :N], wb[:, kt * 256 + 128:kt * 256 + 256], xb[:, kt, :N], kt == 0, kt == KT - 1) for kt in range(KT)],
                     [bw, bxb], [bps[2 + j % 2]])
                w_ = W[j % 2]
                act(w_[:, :N], pg[:, :N], AF.Silu, [bps[j % 2]], [bW[j % 2]])
                tt("dve", HB[:, j * NC:j * NC + N], w_[:, :N], pu[:, :N], ALU.mult, [bW[j % 2], bps[2 + j % 2]], [bARENA])
            for i in range(KT):
                pd = ps[4 + i % 2]
                for hf in range(2):
                    wb, bw = wload("dn%d" % which, l, i * 2 + hf)
                    S.op("pe", [mm(pd[:, :N], wb[:, k * 128:(k + 1) * 128], HB[:, (hf * 11 + k) * NC:(hf * 11 + k) * NC + N],
                                   hf == 0 and k == 0, hf == 1 and k == 10) for k in range(11)], [bw, bARENA], [bps[4 + i % 2]])
                stt(xT[:, i, :N], pd[:, :N], cff, xT[:, i, :N], ALU.mult, ALU.add, [bps[4 + i % 2], bxT], [bxT])

        def w_in(l, N, c0, sample):
            kTo, kio, vo = (kTs, kiTs, vs) if sample else (kTp, kiTp, vp)
            for pc in range(10):
                wb, bw = wload("winf", l, pc)
                for tt_ in range(2):
                    t = pc * 2 + tt_
                    if t > 18 or t >= CFG["ntile"]:
                        continue
                    p_, bp_ = ps[t % 4], bps[t % 4]
                    S.op("pe", [mm(p_[:, :N], wb[:, tt_ * 1024 + kt * 128:tt_ * 1024 + (kt + 1) * 128], xb[:, kt, :N], kt == 0, kt == KT - 1)
                                for kt in range(KT)], [bw, bxb], [bp_])
                    if t < 3:
                        cp("act", wab[:, t, :N], p_[:, :N], [bp_], [bwab])
                        stt(wab[:, t, :N], wab[:, t, :N], -1.0, wab[:, t, :N], ALU.mult, ALU.max, [bwab], [bwab])
                    elif t < 7:
                        cp("act", uT[:, t - 3, :N], p_[:, :N], [bp_], [buT])
                    elif t < 11:
                        cp("act", qT[:, t - 7, :N], p_[:, :N], [bp_], [bqT])
                    elif t < 15:
                        cp("act", kT[:, t - 11, :N], p_[:, :N], [bp_], [bkT])
                        ts("dve", kst[:, :N], p_[:, :N], 1.0, None, ALU.mult, None, [bp_], [bkst])
                        dma(kTo[l, (t - 11) * 128:(t - 10) * 128, c0:c0 + N], kst[:, :N], [bkst], [])
                    elif t < 18:
                        tt("dve", qip[:, t - 15, :N], p_[:, :N], wab[:, t - 15, :N], ALU.mult, [bp_, bwab], [bqip])
                    else:
                        cp("act", ki4[:, :N], p_[:, :N], [bp_], [bki4])
                        cp("dve", kif[:, :N], p_[0:32, :N], [bp_], [bkif])
                        dma(kio[l, :, c0:c0 + N], kif[0:32, :N], [bkif], [])
            if CFG["sub"] < 1:
                return
            wA, bA = wload("wtm", l, 0)
            wB, bB = wload("wtm", l, 1)
            nsub = 16 if sample else N // 128
            for sub in range(nsub):
                m = 8 if sample else 128
                cols = slice(sub * m, (sub + 1) * m)
                pV, pW = ps[4 + sub % 2], ps[6 + sub % 2]
                S.op("pe", [mm(pV[:m, :], xb[:, kt, cols], (wA if kt < 4 else wB)[:, (kt % 4) * 520:(kt % 4) * 520 + 512], kt == 0, kt == KT - 1)
                            for kt in range(KT)], [bA, bB, bxb], [bps[4 + sub % 2]])
                S.op("pe", [mm(pW[:m, 0:8], xb[:, kt, cols], (wA if kt < 4 else wB)[:, (kt % 4) * 520 + 512:(kt % 4) * 520 + 520], kt == 0, kt == KT - 1)
                            for kt in range(KT)], [bA, bB, bxb], [bps[6 + sub % 2]])
                cp("dve", vst[:m, :], pV[:m, :], [bps[4 + sub % 2]], [bvst])
                if sample:
                    dma(vo[l, sub * 8:(sub + 1) * 8, :], vst[:8, :], [bvst], [])
                    S.op("act", lambda e, pV=pV: e.copy(out=Vb[:8, 0, :].rearrange("p (h d) -> p h d", d=65)[:, :, 0:64],
                                                 in_=pV[:8, :].rearrange("p (h d) -> p h d", d=64)), [bps[4 + sub % 2]], [bVb])
                    dma(VSh[sub * 8:(sub + 1) * 8, :], Vb[:8, 0, :], [bVb], [bVSh])
                    act(sgnS[:, sub, :], pW[:8, 0:8], AF.Sign, [bps[6 + sub % 2]], [bsgnS])
                else:
                    dma(vo[l, c0 + sub * 128:c0 + (sub + 1) * 128, :], vst[:, :], [bvst], [])
                    S.op("act", lambda e, pV=pV, sub=sub: e.copy(out=Vb[:, sub, :].rearrange("p (h d) -> p h d", d=65)[:, :, 0:64],
                                                          in_=pV[:, :].rearrange("p (h d) -> p h d", d=64)), [bps[4 + sub % 2]], [bVb])
                    act(sgn[:, sub, :], pW[:, 0:8], AF.Sign, [bps[6 + sub % 2]], [bsgn])
            if CFG["sub"] < 2:
                return
            if not sample:
                dma(kTh[:, c0:c0 + N].rearrange("(i p) n -> p i n", p=128), kT[:, :, :N], [bkT], [bhist])
                dma(kih[:, c0:c0 + N], ki4[:, :N], [bki4], [bhist])
                dma(Vh[c0:c0 + N, :].rearrange("(s p) f -> p s f", p=128), Vb[:, 0:N // 128, :], [bVb], [bhist])

        bhist = Buf(); bVSh = Buf()

        def ssm(l, N, sample):
            nsc = N // 128
            for c in range(4):
                yp, byp = ps[4 + c % 2], bps[4 + c % 2]
                for sc in range(nsc):
                    t0 = sc * 128
                    bre, bim = ps[0 + (sc % 2) * 2], ps[1 + (sc % 2) * 2]
                    bbr, bbi = bps[0 + (sc % 2) * 2], bps[1 + (sc % 2) * 2]
                    S.op("pe", [mm(bre[:, j * 128:(j + 1) * 128], BBre[:, (c * 4 + j) * 128:(c * 4 + j + 1) * 128], uT[:, c, t0:t0 + 128], j == 0, j == 3)
                                for j in range(4)], [bSSM, buT], [bbr])
                    S.op("pe", [mm(bim[:, j * 128:(j + 1) * 128], BBim[:, (c * 4 + j) * 128:(c * 4 + j + 1) * 128], uT[:, c, t0:t0 + 128], j == 0, j == 3)
                                for j in range(4)], [bSSM, buT], [bbi])
                    cs = cosT[:, c * 512:(c + 1) * 512]; sn = sinT[:, c * 512:(c + 1) * 512]
                    tt("dve", W[2][:], bre, cs, ALU.mult, [bbr, bTab], [bW[2]])
                    tt("dve", W[3][:], bim, sn, ALU.mult, [bbi, bTab], [bW[3]])
                    tt("dve", W[0][:], W[2][:], W[3][:], ALU.add, [bW[2], bW[3]], [bW[0]])
                    tt("dve", W[2][:], bim, cs, ALU.mult, [bbi, bTab], [bW[2]])
                    tt("dve", W[3][:], bre, sn, ALU.mult, [bbr, bTab], [bW[3]])
                    tt("dve", W[1][:], W[2][:], W[3][:], ALU.subtract, [bW[2], bW[3]], [bW[1]])
                    if sample:
                        for (w_, bw_, off) in ((W[0], bW[0], 0), (W[1], bW[1], 256)):
                            v = w_[:, :].rearrange("p (j s t) -> p j s t", j=4, s=16)[:, :, :, 0]
                            r_ = RH0[:, off + c * 64:off + (c + 1) * 64].rearrange("p (j s) -> p j s", j=4)
                            tt("dve", v, v, r_, ALU.add, [bw_, bH0], [bw_])
                    for j in range(4):
                        sl = slice(j * 128, (j + 1) * 128)
                        s_ = c * 4 + j
                        ini_re = 0.0 if sample else hpre[:, s_:s_ + 1]
                        ini_im = 0.0 if sample else hpim[:, s_:s_ + 1]
                        S.op("dve", lambda e, sl=sl, s_=s_, ini=ini_re: e.tensor_tensor_scan(out=W[2][:, sl], data0=rdT[:, s_ * 128:(s_ + 1) * 128], data1=W[0][:, sl],
                                                                                       initial=ini, op0=ALU.mult, op1=ALU.add), [bW[0], bTab, bhp], [bW[2]])
                        S.op("dve", lambda e, sl=sl, s_=s_, ini=ini_im: e.tensor_tensor_scan(out=W[3][:, sl], data0=rdT[:, s_ * 128:(s_ + 1) * 128], data1=W[1][:, sl],
                                                                                       initial=ini, op0=ALU.mult, op1=ALU.add), [bW[1], bTab, bhp], [bW[3]])
                    tt("pool", W[4][:], W[2][:], cs, ALU.mult, [bW[2], bTab], [bW[4]])
                    tt("pool", W[5][:], W[3][:], sn, ALU.mult, [bW[3], bTab], [bW[5]])
                    tt("pool", W[0][:], W[4][:], W[5][:], ALU.subtract, [bW[4], bW[5]], [bW[0]])
                    tt("pool", W[4][:], W[3][:], cs, ALU.mult, [bW[3], bTab], [bW[4]])
                    tt("pool", W[5][:], W[2][:], sn, ALU.mult, [bW[2], bTab], [bW[5]])
                    tt("pool", W[1][:], W[4][:], W[5][:], ALU.add, [bW[4], bW[5]], [bW[1]])
                    if sample:
                        for (w_, bw_, off) in ((W[0], bW[0], 0), (W[1], bW[1], 256)):
                            v = w_[:, :].rearrange("p (j s t) -> p j s t", j=4, s=16)[:, :, :, 7]
                            o_ = HSo[:, off + c * 64:off + (c + 1) * 64].rearrange("p (j s) -> p j s", j=4)
                            cp("dve", o_, v, [bw_], [bHS])
                    else:
                        cp("dve", hpre[:, c * 4:c * 4 + 4], W[0][:, :].rearrange("p (j t) -> p j t", j=4)[:, :, 127], [bW[0]], [bhp])
                        cp("dve", hpim[:, c * 4:c * 4 + 4], W[1][:, :].rearrange("p (j t) -> p j t", j=4)[:, :, 127], [bW[1]], [bhp])
                    cp("act", hbre[:], W[0][:], [bW[0]], [bhb])
                    cp("act", hbim[:], W[1][:], [bW[1]], [bhb])
                    fns = []
                    for j in range(4):
                        s_ = c * 4 + j
                        fns.append(mm(yp[:, t0:t0 + 128], Cre[:, s_ * 128:(s_ + 1) * 128], hbre[:, j * 128:(j + 1) * 128], sc == 0 and j == 0, False))
                        fns.append(mm(yp[:, t0:t0 + 128], Cimn[:, s_ * 128:(s_ + 1) * 128], hbim[:, j * 128:(j + 1) * 128], False, False))
                    fns.append(mm(yp[:, t0:t0 + 128], diagD[:, c * 128:(c + 1) * 128], uT[:, c, t0:t0 + 128], False, sc == nsc - 1))
                    S.op("pe", fns, [bSSM, bhb, buT], [byp])
                act(W[4][:, :N], yp[:, :N], AF.Identity, [byp], [bW[4]], scale=0.5)
                act(W[5][:, :N], yp[:, :N], AF.Square, [byp], [bW[5]])
                ts("dve", W[5][:, :N], W[5][:, :N], 2.0 * GK * 0.044715, 2.0 * GK, ALU.mult, ALU.add, [bW[5]], [bW[5]])
                tt("dve", W[5][:, :N], W[5][:, :N], W[4][:, :N], ALU.mult, [bW[5], bW[4]], [bW[5]])
                act(W[5][:, :N], W[5][:, :N], AF.Tanh, [bW[5]], [bW[5]])
                stt(gT[:, c, :N], W[5][:, :N], 1.0, W[4][:, :N], ALU.add, ALU.mult, [bW[5], bW[4]], [bgT])
            wb, bw = wload("wglu", l, 0)
            for m in range(4):
                p_, bp_ = ps[m % 2], bps[m % 2]
                S.op("pe", [mm(p_[:, :N], wb[:, (m * 4 + kt) * 128:(m * 4 + kt + 1) * 128], gT[:, kt, :N], kt == 0, kt == 3) for kt in range(4)], [bw, bgT], [bp_])
                act(W[m % 2][:, :N], p_[:, :N], AF.Sigmoid, [bp_, bPRM], [bW[m % 2]], bias=PRM[:, 48 + m:49 + m])
                tt("dve", ysT[:, m, :N], gT[:, m, :N], W[m % 2][:, :N], ALU.mult, [bgT, bW[m % 2]], [bysT])

        rot = {"k": 0, "v": 0, "kit": 0, "rh": 0, "mb": 0, "pt": 0, "x": 0}

        def qblock(nq, qsl, sgn_ap, bsg, stiles, kblocks, nk, tri_ap, dst):
            I_ = I4 if nq == 128 else I8
            for h in range(8):
                ts("pool", sd[:nq, h, :nq], identf[:nq, :nq], sgn_ap[:, h:h + 1], None, ALU.mult, None, [bCST, bsg], [bsd])
            off = 0
            for (w, get) in stiles:
                kt_, bk_ = get()
                sp_, bsp_ = ps[3], bps[3]
                for h in range(8):
                    xi = rot["x"] % 3; rot["x"] += 1
                    ri = rot["rh"] % 3; rot["rh"] += 1
                    j = h % 3
                    S.op("pe", [mm(ps[xi][:nq, :w], qip[32 * j:32 * j + 32, h // 3, qsl], kt_[32 * j:32 * j + 32, :w], True, True)], [bqip, bk_], [bps[xi]])
                    act(Rh[ri][:nq, :w], ps[xi][:nq, :w], AF.Relu, [bps[xi]], [bRh[ri]])
                    S.op("pe", [mm(sp_[:nq, :w], sd[:nq, h, :nq], Rh[ri][:nq, :w], h == 0, h == 7)], [bsd, bRh[ri]], [bsp_])
                cp("dve", ARENA[:nq, off:off + w], sp_[:nq, :w], [bsp_], [bARENA])
                off += w
            kwl = kblocks[-1][0]
            tt("dve", ARENA[:nq, nk - kwl:nk], ARENA[:nq, nk - kwl:nk], tri_ap, ALU.add, [bARENA, bCST], [bARENA])
            S.op("dve", lambda e: e.memset(lo[:], -2048.0), [], [bbis])
            if nk > 256:
                for it in range(NITER):
                    wk = 2048.0 / (2.0 ** it)
                    ts("dve", mid[:nq], lo[:nq], wk, None, ALU.add, None, [bbis], [bbis])
                    npc = (nk + 2047) // 2048
                    for pc in range(npc):
                        a, b = pc * 2048, min(nk, (pc + 1) * 2048)
                        s2 = None if pc == 0 else cnt[:nq, (pc - 1) % 2:(pc - 1) % 2 + 1]
                        ts("dve", junk[:nq, :b - a], ARENA[:nq, a:b], mid[:nq], s2, ALU.is_gt, ALU.add, [bARENA, bbis], [bjunk, bbis],
                           accum=cnt[:nq, pc % 2:pc % 2 + 1])
                    cl = cnt[:nq, (npc - 1) % 2:(npc - 1) % 2 + 1]
                    ts("dve", tq[:nq], cl, 255.5, wk, ALU.is_gt, ALU.mult, [bbis], [bbis])
                    tt("dve", lo[:nq], lo[:nq], tq[:nq], ALU.add, [bbis], [bbis])
            O = [ps[6], ps[7]]; bO = [bps[6], bps[7]]
            LT = PS[:, 4 * 512:6 * 512]; bLT = [bps[4], bps[5]]
            off = 0
            nb = len(kblocks)
            for bi, (kw, get) in enumerate(kblocks):
                kk, bkk, vv, bvv = get()
                mi = rot["mb"] % 2; rot["mb"] += 1
                pi = rot["pt"] % 2; rot["pt"] += 1
                ts("dve", mb[mi][:nq, :kw], ARENA[:nq, off:off + kw], lo[:nq], NEG, ALU.is_le, ALU.mult, [bARENA, bbis], [bmb[mi]])
                fns = []
                for hf in range(2):
                    fns.append(mm(LT[:kw, hf * 512:hf * 512 + 4 * nq], mb[mi][:nq, :kw], I_[:nq, :4 * nq], True, False))
                for hh in range(4):
                    for hf in range(2):
                        h = hh * 2 + hf
                        fns.append(mm(LT[:kw, hf * 512 + hh * nq:hf * 512 + (hh + 1) * nq], kk[64 * hf:64 * hf + 64, hh, :kw],
                                      qT[64 * hf:64 * hf + 64, hh, qsl], False, hh == 3))
                S.op("pe", fns, [bmb[mi], bI, bkk, bqT], bLT)
                for hf in range(2):
                    act(PT[pi][:kw, hf * 512:hf * 512 + 4 * nq], LT[:kw, hf * 512:hf * 512 + 4 * nq], AF.Exp, bLT, [bPT[pi]], scale=0.125)
                fns = []
                for h in range(8):
                    hf, hh = h % 2, h // 2
                    fns.append(mm(O[h // 4][:nq, (h % 4) * 65:(h % 4) * 65 + 65], PT[pi][:kw, hf * 512 + hh * nq:hf * 512 + (hh + 1) * nq],
                                  vv[:kw, h * 65:h * 65 + 65], bi == 0 and h % 4 == 0, bi == nb - 1 and h % 4 == 3))
                S.op("pe", fns, [bPT[pi], bvv], bO)
                off += kw
            for h in range(8):
                o_ = O[h // 4]
                S.op("dve", lambda e, o_=o_, h=h: e.reciprocal(out=rdn[:nq, h:h + 1], in_=o_[:nq, (h % 4) * 65 + 64:(h % 4) * 65 + 65]), bO, [brdn])
                ts("dve", yat[:nq, h * 64:(h + 1) * 64], o_[:nq, (h % 4) * 65:(h % 4) * 65 + 64], rdn[:nq, h:h + 1], None, ALU.mult, None, bO + [brdn], [byat])
            for i in range(4):
                xi = rot["x"] % 3; rot["x"] += 1
                S.op("pe", [lambda e, xi=xi, i=i: e.transpose(out=ps[xi][:, :nq], in_=yat[:nq, i * 128:(i + 1) * 128], identity=identf[:nq, :nq])], [byat, bCST], [bps[xi]])
                cp("act", dst(i), ps[xi][:, :nq], [bps[xi]], [byattT])

        def prompt_attention(N, c0):
            for sub in range(N // 128):
                qb = (c0 + sub * 128) // 128
                nk = 128 * (qb + 1)
                stiles = []
                for st in range((nk + 511) // 512):
                    w = min(512, nk - st * 512)

                    def get(st=st, w=w):
                        i = rot["kit"] % 2; rot["kit"] += 1
                        dma(kit[i][:, :w], kih[:, st * 512:st * 512 + w], [bhist], [bkit[i]], q=dq())
                        return kit[i], bkit[i]
                    stiles.append((w, get))
                kblocks = []
                for sb_ in range(qb + 1):
                    def getk(sb_=sb_):
                        i = rot["k"] % 2; rot["k"] += 1
                        dma(kTt[i][:, :, :], kTh[:, sb_ * 128:(sb_ + 1) * 128].rearrange("(i p) n -> p i n", p=128), [bhist], [bkTt[i]], q=dq())
                        dma(Vt[i][:, :], Vh[sb_ * 128:(sb_ + 1) * 128, :], [bhist], [bVt[i]], q=dq())
                        return kTt[i], bkTt[i], Vt[i], bVt[i]
                    kblocks.append((128, getk))
                qblock(128, slice(sub * 128, (sub + 1) * 128), sgn[:, sub, :], bsgn, stiles, kblocks, nk, tri,
                       lambda i, sub=sub: yattT[:, i, sub * 128:(sub + 1) * 128])

        def sample_attention(l):
            for seq in range(16):
                for pg in range(16):
                    S.dma("pool", lambda e, pg=pg, seq=seq: e.indirect_dma_start(
                        out=kip[:, pg, :], out_offset=None, in_=cki[:, :],
                        in_offset=bass.IndirectOffsetOnAxis(ap=IDX[:, seq * 16 + pg:seq * 16 + pg + 1], axis=0)), [bIDX], [bkip])
                for g in range(4):
                    xi = rot["x"] % 3; rot["x"] += 1
                    S.op("pe", [lambda e, xi=xi, g=g, k=k: e.transpose(out=ps[xi][0:32, k * 128:(k + 1) * 128], in_=kip[:, g * 4 + k, :], identity=identf)
                                for k in range(4)], [bkip, bCST], [bps[xi]])
                    for r in range(4):
                        cp("act" if r % 2 else "dve", kitS[32 * r:32 * r + 32, g * 512:(g + 1) * 512], ps[xi][0:32, :], [bps[xi]], [bkitS])
                cp("dve", kitS[:, 2048:2056], ki4[:, seq * 8:(seq + 1) * 8], [bki4], [bkitS])
                stiles = []
                for st in range(5):
                    w = 512 if st < 4 else 8
                    stiles.append((w, lambda st=st, w=w: (kitS[:, st * 512:st * 512 + w], bkitS)))
                kblocks = []
                for pg in range(16):
                    def getk(pg=pg, seq=seq):
                        i = rot["k"] % 2; rot["k"] += 1
                        S.dma("pool", lambda e: e.indirect_dma_start(out=kpg[i][:, :], out_offset=None, in_=ck[:, :],
                              in_offset=bass.IndirectOffsetOnAxis(ap=IDX[:, seq * 16 + pg:seq * 16 + pg + 1], axis=0)), [bIDX], [bkpg[i]])
                        S.dma("pool", lambda e: e.indirect_dma_start(out=vpg[i][:, :], out_offset=None, in_=cv[:, :],
                              in_offset=bass.IndirectOffsetOnAxis(ap=IDX[:, seq * 16 + pg:seq * 16 + pg + 1], axis=0)), [bIDX], [bvpg[i]])
                        xi = rot["x"] % 3; rot["x"] += 1
                        S.op("pe", [lambda e, k=k: e.transpose(out=ps[xi][:, k * 128:(k + 1) * 128], in_=kpg[i][:, k * 128:(k + 1) * 128], identity=identf)
                                    for k in range(4)], [bkpg[i], bCST], [bps[xi]])
                        cp("act", kTt[i][:, :, :].rearrange("p i n -> p (i n)"), ps[xi][:, :], [bps[xi]], [bkTt[i]])
                        S.op("pool", lambda e: e.tensor_copy(out=Vt[i][:, :].rearrange("p (h d) -> p h d", d=65)[:, :, 0:64],
                                                      in_=vpg[i][:, :].rearrange("p (h d) -> p h d", d=64)), [bvpg[i]], [bVt[i]])
                        return kTt[i], bkTt[i], Vt[i], bVt[i]
                    kblocks.append((128, getk))

                def getn(seq=seq):
                    i = rot["k"] % 2; rot["k"] += 1
                    cp("dve", kTt[i][:, :, 0:8], kT[:, :, seq * 8:(seq + 1) * 8], [bkT], [bkTt[i]])
                    dma(Vt[i][0:8, :], VSh[seq * 8:(seq + 1) * 8, :], [bVSh], [bVt[i]])
                    return kTt[i], bkTt[i], Vt[i], bVt[i]
                kblocks.append((8, getn))
                qblock(8, slice(seq * 8, (seq + 1) * 8), sgnS[:, seq, :], bsgnS, stiles, kblocks, 2056, CST[0:8, 128:136],
                       lambda i, seq=seq: yattT[:, i, seq * 8:(seq + 1) * 8])

        def w_out(l, N):
            for pc in range(4):
                wb, bw = wload("wout", l, pc)
                for mm_ in range(2):
                    i = pc * 2 + mm_
                    p_, bp_ = ps[i % 2], bps[i % 2]
                    S.op("pe", [mm(p_[:, :N], wb[:, mm_ * 1024 + kt * 128:mm_ * 1024 + (kt + 1) * 128], (ysT if kt < 4 else yattT)[:, kt % 4, :N], kt == 0, kt == KT - 1)
                                for kt in range(KT)], [bw, bysT, byattT], [bp_])
                    stt(xT[:, i, :N], p_[:, :N], 1.0 / ALPHA, xT[:, i, :N], ALU.mult, ALU.add, [bp_, bxT], [bxT])

        bx1 = [Buf() for _ in range(NCH + 1)]
        ST = CFG["stop"]
        precast()
        for l in range(CFG["layers"]):
            if ST > 0:
                layer_setup(l)
            for ci in list(range(CFG["nch"])) + ([NCH] if CFG["sample"] else []):
                sample = ci == NCH
                N = NS if sample else NC
                c0 = 0 if sample else ci * NC
                if sample:
                    build_tables(taus, nfs)
                    cp("dve", IDF[:], PTB[:], [bIDX], [bIDX])
                    ts("dve", IDF[:], IDF[:], 128.0, float(l * 2560 * 128), ALU.mult, ALU.add, [bIDX], [bIDX])
                    ts("dve", IDF[:], IDF[:], iotaf, None, ALU.add, None, [bIDX, bCST], [bIDX])
                    cp("dve", IDX[:], IDF[:], [bIDX], [bIDX])
                src = (xTs if sample else xTp) if l == 0 else (x1s if sample else x1p)
                if ST > 1:
                    dma(xT[:, :, :N], src[:, c0:c0 + N].rearrange("(k p) n -> p k n", p=128), [bx1[ci]], [bxT])
                    for kt in range(KT):
                        cp("pool", xb[:, kt, :N], xT[:, kt, :N], [bxT], [bxb])
                if ST > 2:
                    ffn(l, 0, N)
                if ST > 3:
                    layernorm(l, 0, N)
                if ST > 4:
                    w_in(l, N, c0, sample)
                if ST > 5:
                    ssm(l, N, sample)
                if ST > 6:
                    if sample:
                        sample_attention(l)
                        dma(hs[l], HSo[:], [bHS], [])
                    else:
                        prompt_attention(N, c0)
                if ST > 7:
                    w_out(l, N)
                    layernorm(l, 1, N)
                    ffn(l, 1, N)
                    layernorm(l, 2, N)
                dst = (x1s if sample else x1p) if l == 0 else (yTs if sample else yTp)
                if ST > 1:
                    dma(dst[:, c0:c0 + N].rearrange("(k p) n -> p k n", p=128), xT[:, :, :N], [bxT], [bx1[ci]])
                if ci == CFG["nch"] - 1:
                    dma(hp[l, :, 0:16], hpre[:], [bhp], [])
                    dma(hp[l, :, 16:32], hpim[:], [bhp], [])
        S.finish()

        with nc.Block() as block:
            @block.tensor
            def _(e):
                for f in S.ops["pe"]:
                    f(e)

            @block.scalar
            def _(e):
                for f in S.ops["act"]:
                    f(e)

            @block.vector
            def _(e):
                for f in S.ops["dve"]:
                    f(e)

            @block.gpsimd
            def _(e):
                for f in S.ops["pool"]:
                    f(e)

            @block.sync
            def _(e):
                for f in S.ops["sp"]:
                    f(e)
    return nc, {e: len(v) for e, v in S.ops.items()}


def _prep(inp):
    f = np.float32
    g = {}
    def C(a):
        return np.ascontiguousarray(a, dtype=f)
    for w, name in ((0, "ffn1_up"), (1, "ffn2_up")):
        W = inp[name].reshape(L, KT, 128, 2, FT, 128)
        g["up%d" % (w + 1)] = C(W.transpose(0, 4, 2, 1, 3, 5).reshape(L, FT, 128, 2048))
    for w, name in ((0, "ffn1_down"), (1, "ffn2_down")):
        W = inp[name].reshape(L, 2, 11, 128, 8, 128)
        g["dn%d" % (w + 1)] = C(W.transpose(0, 4, 1, 3, 2, 5).reshape(L, 16, 128, 1408))
    win = inp["w_in"]
    u, q, k, v = win[:, :, 0:512], win[:, :, 512:1024], win[:, :, 1024:1536], win[:, :, 1536:2048]
    qi, ki, wi = win[:, :, 2048:2304], win[:, :, 2304:2336], win[:, :, 2336:2344]
    wexp = np.repeat(wi, 32, axis=2)
    ki4 = np.tile(ki, (1, 1, 4))
    qi3 = np.zeros((L, D, 384), f); we3 = np.zeros((L, D, 384), f)
    for h in range(8):
        o = (h // 3) * 128 + (h % 3) * 32
        qi3[:, :, o:o + 32] = qi[:, :, h * 32:(h + 1) * 32]
        we3[:, :, o:o + 32] = wexp[:, :, h * 32:(h + 1) * 32]
    fm = np.concatenate([we3, u, q, k, qi3, ki4, np.zeros((L, D, 128), f)], axis=2)
    fm = fm.reshape(L, KT, 128, 10, 2, 128)
    g["winf"] = C(fm.transpose(0, 3, 2, 4, 1, 5).reshape(L, 10, 128, 2048))
    tm = np.concatenate([v, wi], axis=2).reshape(L, 2, 4, 128, 520)
    g["wtm"] = C(tm.transpose(0, 1, 3, 2, 4).reshape(L, 2, 128, 2080))
    g["wglu"] = C(inp["w_glu"].reshape(L, 4, 128, 4, 128).transpose(0, 2, 3, 1, 4).reshape(L, 128, 2048))
    wo = inp["w_out"].reshape(L, KT, 128, 4, 2, 128)
    g["wout"] = C(wo.transpose(0, 3, 2, 4, 1, 5).reshape(L, 4, 128, 2048))
    prm = np.zeros((L, 128, 56), f)
    prm[:, :, 0:24] = inp["ln_g"].reshape(L, 3, KT, 128).transpose(0, 3, 1, 2).reshape(L, 128, 24)
    prm[:, :, 24:48] = inp["ln_b"].reshape(L, 3, KT, 128).transpose(0, 3, 1, 2).reshape(L, 128, 24)
    prm[:, :, 48:52] = inp["b_glu"].reshape(L, 4, 128).transpose(0, 2, 1)
    prm[:, :, 52:56] = inp["ssm_d"].reshape(L, 4, 128).transpose(0, 2, 1)
    g["prm"] = prm
    st = lambda a: a.reshape(L, 16, 128).transpose(0, 2, 1)
    lst = np.repeat(inp["ssm_log_step"][:, :, None], 64, axis=2)
    g["sS"] = C(np.concatenate([st(inp["ssm_lambda_re"]), st(inp["ssm_lambda_im"]), st(lst)], axis=2))
    def bbl(a):
        x = a.reshape(L, 4, 4, 128)
        x = x.transpose(0, 2, 1, 3)[:, :, None]
        return np.broadcast_to(x, (L, 4, 32, 4, 128)).reshape(L, 128, 512)
    def bexp(b):
        o = np.zeros((L, 4, 2, 16, 4, 2, 64), f)
        bb = b.reshape(L, 4, 4, 2, 64, 16)
        for g2 in range(2):
            o[:, :, g2, :, :, g2, :] = bb[:, :, :, g2].transpose(0, 2, 4, 1, 3)
        return o.reshape(L, 128, 512)
    def zl(a):
        x = a.reshape(L, 4, 1, 512)
        return np.broadcast_to(x, (L, 4, 128, 512))
    def bz(be):
        x = be.reshape(L, 4, 32, 4, 128)
        o = np.zeros((L, 4, 4, 32, 4, 128), f)
        for j in range(4):
            o[:, :, j, :, j, :] = x[:, j].transpose(0, 2, 1, 3)
        return o.reshape(L, 4, 128, 512)
    g["sB"] = C(np.concatenate([zl(inp["ssm_lambda_re"].reshape(L, 16, 128)), zl(inp["ssm_lambda_im"].reshape(L, 16, 128)),
                                zl(lst.reshape(L, 16, 128)), bz(bexp(inp["ssm_b_re"])), bz(bexp(inp["ssm_b_im"]))], axis=3))
    def cexp(cm):
        o = np.zeros((L, 2, 64, 16, 4, 2, 16), f)
        cc = cm.reshape(L, 16, 2, 16, 64)
        for s in range(16):
            for g2 in range(2):
                o[:, g2, :, s, s % 4, g2, :] = cc[:, s, g2].transpose(0, 2, 1)
        return o.reshape(L, 128, 2048)
    g["sC"] = C(np.concatenate([cexp(inp["ssm_c_re"]), cexp(inp["ssm_c_im"])], axis=2))
    ck = inp["cache_k"].reshape(L * 2560 * 128, 512)
    cv = inp["cache_v"].reshape(L * 2560 * 128, 512)
    cki = inp["cache_kidx"].reshape(L * 2560 * 128, 32)
    cst = np.zeros((128, 897), f)
    cst[:, 0:128] = np.eye(128)
    cst[:, 128:256] = np.where(np.arange(128)[None, :] > np.arange(128)[:, None], NEG, 0.0)
    cst[:, 256:384] = np.arange(1, 129)[None, :]
    cst[:, 384:512] = np.tile(np.arange(1, 9), 16)[None, :]
    cst[:, 512:640] = 1.0; cst[:, 512] = 0.0; cst[:, 512:640][:, 0] = 1.0
    nfs = np.ones(128, f); nfs[0::8] = 0.0
    cst[:, 640:768] = nfs[None, :]
    cst[:, 768:896] = 1.0 / 1024.0
    cst[:, 896] = np.arange(128)
    g["cst"] = cst
    cores = []
    for c in range(8):
        b = c % 2
        d = dict(g)
        d["xTp"] = C(inp["x_prompt"][b].T)
        d["xTs"] = C(inp["x_sample"][16 * c:16 * c + 16].reshape(128, D).T)
        h0 = np.zeros((L, 128, 512), f)
        for i, nm in enumerate(("state_ssm_re", "state_ssm_im")):
            a = inp[nm][:, 16 * c:16 * c + 16].reshape(L, 16, 16, 128)
            h0[:, :, i * 256:(i + 1) * 256] = a.transpose(0, 3, 2, 1).reshape(L, 128, 256)
        d["h0"] = h0
        d["ck"], d["cv"], d["cki"] = ck, cv, cki
        d["ptb"] = np.ascontiguousarray(np.broadcast_to(inp["page_table"][16 * c:16 * c + 16].reshape(1, 256), (128, 256)).astype(np.int32))
        cores.append(d)
    return cores


_NC = None


def kernel(**inputs):
    global _NC
    inp = {k: np.asarray(v) for k, v in inputs.items()}
    if _NC is None:
        _NC = build()[0]
    in_maps = _prep(inp)
    res = run_bass_kernel_spmd(_NC, in_maps, core_ids=list(range(8))).results
    f = np.float32
    B = 2
    y_p = np.stack([res[b]["yTp"].T for b in range(B)]).astype(f)
    y_s = np.concatenate([res[c]["yTs"].T.reshape(16, 8, D) for c in range(8)]).astype(f)
    k_p = np.stack([np.stack([res[b]["kTp"][l].T.reshape(NP, 8, 64) for b in range(B)]) for l in range(L)]).astype(f)
    v_p = np.stack([np.stack([res[b]["vp"][l].reshape(NP, 8, 64) for b in range(B)]) for l in range(L)]).astype(f)
    ki_p = np.stack([np.stack([res[b]["kiTp"][l].T for b in range(B)]) for l in range(L)]).astype(f)
    def hst(a):
        return a.T.reshape(32, 64)
    hr_p = np.stack([np.stack([hst(res[b]["hp"][l][:, 0:16]) for b in range(B)]) for l in range(L)]).astype(f)
    hi_p = np.stack([np.stack([hst(res[b]["hp"][l][:, 16:32]) for b in range(B)]) for l in range(L)]).astype(f)
    k_s = np.stack([np.concatenate([res[c]["kTs"][l].T.reshape(16, 8, 8, 64) for c in range(8)]) for l in range(L)]).astype(f)
    v_s = np.stack([np.concatenate([res[c]["vs"][l].reshape(16, 8, 8, 64) for c in range(8)]) for l in range(L)]).astype(f)
    ki_s = np.stack([np.concatenate([res[c]["kiTs"][l].T.reshape(16, 8, 32) for c in range(8)]) for l in range(L)]).astype(f)
    def hss(a):
        return a.reshape(128, 16, 16).transpose(2, 1, 0).reshape(16, 32, 64)
    hr_s = np.stack([np.concatenate([hss(res[c]["hs"][l][:, 0:256]) for c in range(8)]) for l in range(L)]).astype(f)
    hi_s = np.stack([np.concatenate([hss(res[c]["hs"][l][:, 256:512]) for c in range(8)]) for l in range(L)]).astype(f)
    return (y_p, y_s, k_p, v_p, ki_p, hr_p, hi_p, k_s, v_s, ki_s, hr_s, hi_s)
```

```python
import math
from contextlib import ExitStack
import numpy as np
import concourse.bass as bass
import concourse.mybir as mybir
from concourse.bass_utils import run_bass_kernel_spmd

F32 = mybir.dt.float32
BF16 = mybir.dt.bfloat16
I32 = mybir.dt.int32
ALU = mybir.AluOpType
AF = mybir.ActivationFunctionType

L = 2
D = 1024
KT = 8
FT = 22
NP = 8192
NS = 128
NC = 256
NCH = NP // NC
ALPHA = (2.0 * L) ** 0.25
EPS2 = 1e-5 / (ALPHA * ALPHA)
NEG = -30000.0
NITER = 24
TWO_PI = 2.0 * math.pi
GK = math.sqrt(2.0 / math.pi)


class Buf:
    __slots__ = ("w", "r", "x")

    def __init__(self, x=False):
        self.w = None
        self.r = {}
        self.x = x


class Sched:
    def __init__(self, sems, dsems):
        self.E = ["pe", "act", "dve", "pool", "sp"]
        self.ops = {e: [] for e in self.E}
        self.cnt = {e: 0 for e in self.E}
        self.sem = sems
        self.dsems = dsems
        self.seen = {e: {} for e in self.E}
        self.duse = [0] * len(dsems)
        self.di = 0

    def _wait(self, eng, key, val):
        if key == ("e", "pe") and eng == "pe":
            return
        if self.seen[eng].get(key, 0) >= val:
            return
        self.seen[eng][key] = val
        sem = self.sem[key[1]] if key[0] == "e" else self.dsems[key[1]]
        self.ops[eng].append(lambda e, sem=sem, val=val: e.wait_ge(sem, val))

    def _deps(self, eng, reads, writes):
        for b in reads:
            if b.w:
                self._wait(eng, *b.w)
            if b.x:
                for k, v in b.r.items():
                    if k != ("e", eng):
                        self._wait(eng, k, v)
        for b in writes:
            if b.w:
                self._wait(eng, *b.w)
            for k, v in b.r.items():
                self._wait(eng, k, v)

    def _mark(self, tok, reads, writes):
        k, v = tok
        for b in writes:
            b.w = tok
            b.r = {}
        for b in reads:
            if b.r.get(k, 0) < v:
                b.r[k] = v

    def op(self, eng, fns, reads=(), writes=()):
        if not isinstance(fns, (list, tuple)):
            fns = [fns]
        self._deps(eng, reads, writes)
        self.cnt[eng] += 1
        tok = (("e", eng), self.cnt[eng])
        for f in fns[:-1]:
            self.ops[eng].append(f)
        self.ops[eng].append(lambda e, f=fns[-1], sem=self.sem[eng]: f(e).then_inc(sem, 1))
        self._mark(tok, reads, writes)

    def dma(self, q, fn, reads=(), writes=()):
        self._deps(q, reads, writes)
        slot = self.di % len(self.dsems)
        self.di += 1
        if self.duse[slot] > 0:
            self._wait(q, ("d", slot), 16 * self.duse[slot])
        self.duse[slot] += 1
        tok = (("d", slot), 16 * self.duse[slot])
        self.ops[q].append(lambda e, fn=fn, sem=self.dsems[slot]: fn(e).then_inc(sem, 16))
        self._mark(tok, reads, writes)

    def finish(self):
        for slot, u in enumerate(self.duse):
            if u > 0:
                self._wait("sp", ("d", slot), 16 * u)
        for e in ["pe", "act", "dve", "pool"]:
            if self.cnt[e] > 0:
                self._wait("sp", ("e", e), self.cnt[e])


CFG = {"layers": L, "nch": NCH, "sample": True, "rows": L * 2560 * 128, "dbg": False, "stop": 99, "sub": 99, "ntile": 99}


def build():
    nc = bass.Bass("TRN2", target_bir_lowering=False)

    def din(name, shape, d=F32):
        return nc.dram_tensor(name, list(shape), d, kind="ExternalInput").ap()

    def dout(name, shape):
        return nc.dram_tensor(name, list(shape), F32, kind="ExternalOutput").ap()

    def dscr(name, shape, d):
        return nc.dram_tensor(name, list(shape), d, kind="Internal").ap()

    xTp = din("xTp", [D, NP]); xTs = din("xTs", [D, NS])
    up = [din("up1", [L, FT, 128, 2048]), din("up2", [L, FT, 128, 2048])]
    dn = [din("dn1", [L, 16, 128, 1408]), din("dn2", [L, 16, 128, 1408])]
    winf = din("winf", [L, 10, 128, 2048])
    wtm = din("wtm", [L, 2, 128, 2080])
    wglu = din("wglu", [L, 128, 2048])
    wout = din("wout", [L, 4, 128, 2048])
    prm = din("prm", [L, 128, 56])
    sS = din("sS", [L, 128, 48])
    sB = din("sB", [L, 4, 128, 2560])
    sC = din("sC", [L, 128, 4096])
    h0 = din("h0", [L, 128, 512])
    ck = din("ck", [CFG["rows"], 512]); cv = din("cv", [CFG["rows"], 512]); cki = din("cki", [CFG["rows"], 32])
    ptb = din("ptb", [128, 256], I32)
    cst = din("cst", [128, 897])

    yTp = dout("yTp", [D, NP]); yTs = dout("yTs", [D, NS])
    kTp = dout("kTp", [L, 512, NP]); vp = dout("vp", [L, NP, 512]); kiTp = dout("kiTp", [L, 32, NP])
    hp = dout("hp", [L, 128, 32])
    kTs = dout("kTs", [L, 512, NS]); vs = dout("vs", [L, NS, 512]); kiTs = dout("kiTs", [L, 32, NS])
    hs = dout("hs", [L, 128, 512])
    x1p = dout("x1p", [D, NP]) if CFG["dbg"] else dscr("x1p", [D, NP], F32)
    x1s = dout("x1s", [D, NS]) if CFG["dbg"] else dscr("x1s", [D, NS], F32)
    kTh = dscr("kTh", [512, NP], BF16); kih = dscr("kih", [128, NP], BF16); Vh = dscr("Vh", [NP, 520], BF16)
    VSh = dscr("VSh", [NS, 520], BF16)
    wsc = {"up0": dscr("wup0", [L, FT, 128, 2048], BF16), "up1": dscr("wup1", [L, FT, 128, 2048], BF16),
           "dn0": dscr("wdn0", [L, 16, 128, 1408], BF16), "dn1": dscr("wdn1", [L, 16, 128, 1408], BF16),
           "winf": dscr("wwinf", [L, 10, 128, 2048], BF16), "wtm": dscr("wwtm", [L, 2, 128, 2080], BF16),
           "wglu": dscr("wwglu", [L, 1, 128, 2048], BF16), "wout": dscr("wwout", [L, 4, 128, 2048], BF16)}
    wsrc = {"up0": up[0], "up1": up[1], "dn0": dn[0], "dn1": dn[1], "winf": winf, "wtm": wtm,
            "wglu": wglu.rearrange("l (o p) n -> l o p n", o=1), "wout": wout}
    wnp = {"up0": (FT, 2048), "up1": (FT, 2048), "dn0": (16, 1408), "dn1": (16, 1408), "winf": (10, 2048), "wtm": (2, 2080), "wglu": (1, 2048), "wout": (4, 2048)}

    es = ExitStack()
    with es:
        def sb(name, shape, d=F32):
            return es.enter_context(nc.sbuf_tensor(name, list(shape), d))

        sems = {e: es.enter_context(nc.semaphore("s_" + e)) for e in ["pe", "act", "dve", "pool"]}
        dsems = [es.enter_context(nc.semaphore("d%d" % i)) for i in range(24)]
        S = Sched(sems, dsems)

        ARENA = sb("arena", [128, 8192]); bARENA = Buf()
        HB = ARENA[:].bitcast(BF16)
        AI = ARENA[:].bitcast(I32)
        xT = sb("xT", [128, KT, NC]); bxT = Buf()
        xb = sb("xb", [128, KT, NC], BF16); bxb = Buf()
        wst = sb("wst", [128, 2080]); bwst = Buf()
        NWB = 3
        wbf = [sb("wbf%d" % i, [128, 2080], BF16) for i in range(NWB)]; bwbf = [Buf() for _ in range(NWB)]
        CST = sb("CST", [128, 897]); bCST = Buf()
        identf = CST[:, 0:128]; tri = CST[:, 128:256]; taur = CST[:, 256:384]; taus = CST[:, 384:512]
        nfp = CST[:, 512:640]; nfs = CST[:, 640:768]; onesf = CST[:, 768:896]; iotaf = CST[:, 896:897]
        I4 = sb("I4", [128, 512], BF16); I8 = sb("I8", [8, 32], BF16); bI = Buf()
        PRM = sb("PRM", [128, 56]); bPRM = Buf()
        SS = sb("SS", [128, 48]); R16 = sb("R16", [128, 16]); ANG = sb("ANG", [128, 16]); STP = sb("STP", [128, 16]); bSS = Buf()
        cosT = sb("cosT", [128, 2048]); sinT = sb("sinT", [128, 2048]); rdT = sb("rdT", [128, 2048]); bTab = Buf()
        BBre = sb("BBre", [128, 2048], BF16); BBim = sb("BBim", [128, 2048], BF16)
        Cre = sb("Cre", [128, 2048], BF16); Cimn = sb("Cimn", [128, 2048], BF16)
        diagD = sb("diagD", [128, 512], BF16); bSSM = Buf()
        hpre = sb("hpre", [128, 16]); hpim = sb("hpim", [128, 16]); bhp = Buf()
        H0 = sb("H0", [128, 512]); RH0 = sb("RH0", [128, 512]); HSo = sb("HSo", [128, 512]); bH0 = Buf(); bHS = Buf()
        uT = sb("uT", [128, 4, NC], BF16); buT = Buf()
        qT = sb("qT", [128, 4, NC], BF16); bqT = Buf()
        kT = sb("kT", [128, 4, NC], BF16); bkT = Buf()
        kst = sb("kst", [128, NC]); bkst = Buf()
        wab = sb("wab", [128, 3, NC]); bwab = Buf()
        qip = sb("qip", [128, 3, NC], BF16); bqip = Buf()
        ki4 = sb("ki4", [128, NC], BF16); bki4 = Buf()
        kif = sb("kif", [32, NC]); bkif = Buf()
        vst = sb("vst", [128, 512]); bvst = Buf()
        Vb = sb("Vb", [128, NC // 128, 520], BF16); bVb = Buf()
        sgn = sb("sgn", [128, NC // 128, 8]); bsgn = Buf()
        sgnS = sb("sgnS", [8, 16, 8]); bsgnS = Buf()
        W = [sb("W%d" % i, [128, 512]) for i in range(6)]; bW = [Buf() for _ in range(6)]
        hbre = sb("hbre", [128, 512], BF16); hbim = sb("hbim", [128, 512], BF16); bhb = Buf()
        gT = sb("gT", [128, 4, NC], BF16); bgT = Buf()
        ysT = sb("ysT", [128, 4, NC], BF16); bysT = Buf()
        yattT = sb("yattT", [128, 4, NC], BF16); byattT = Buf()
        kit = [sb("kit%d" % i, [128, 512], BF16) for i in range(2)]; bkit = [Buf(), Buf()]
        kTt = [sb("kTt%d" % i, [128, 4, 128], BF16) for i in range(2)]; bkTt = [Buf(), Buf()]
        Vt = [sb("Vt%d" % i, [128, 520], BF16) for i in range(2)]; bVt = [Buf(), Buf()]
        Rh = [sb("Rh%d" % i, [128, 512], BF16) for i in range(3)]; bRh = [Buf() for _ in range(3)]
        sd = sb("sd", [128, 8, 128], BF16); bsd = Buf()
        mb = [sb("mb%d" % i, [128, 128], BF16) for i in range(2)]; bmb = [Buf(), Buf()]
        PT = [sb("PT%d" % i, [128, 1024], BF16) for i in range(2)]; bPT = [Buf(), Buf()]
        yat = sb("yat", [128, 512]); byat = Buf()
        rdn = sb("rdn", [128, 8]); brdn = Buf()
        lo = sb("lo", [128, 1]); mid = sb("mid", [128, 1]); cnt = sb("cnt", [128, 2]); tq = sb("tq", [128, 1]); bbis = Buf()
        junk = sb("junk", [128, 2048], BF16); bjunk = Buf()
        IDX = sb("IDX", [128, 256], I32); PTB = sb("PTB", [128, 256], I32); IDF = sb("IDF", [128, 256]); bIDX = Buf()
        kip = sb("kip", [128, 16, 32]); bkip = Buf()
        kitS = sb("kitS", [128, 2056], BF16); bkitS = Buf()
        kpg0 = sb("kpg0", [128, 512]); kpg = [kpg0, kpg0]; bkpg0 = Buf(); bkpg = [bkpg0, bkpg0]
        vpg0 = sb("vpg0", [128, 512]); vpg = [vpg0, vpg0]; bvpg0 = Buf(); bvpg = [bvpg0, bvpg0]
        PS = es.enter_context(nc.psum_tensor("PS", [128, 4096], F32))
        ps = [PS[:, i * 512:(i + 1) * 512] for i in range(8)]; bps = [Buf(True) for _ in range(8)]

        dcount = [0]

        def dq():
            dcount[0] += 1
            return "sp" if dcount[0] % 2 else "pool"

        def dma(out, in_, reads, writes, q=None):
            S.dma(q or "sp", lambda e: e.dma_start(out=out, in_=in_), reads, writes)

        def mm(out, lhsT, rhs, st, sp):
            return lambda e: e.matmul(out, lhsT, rhs, start=st, stop=sp, skip_group_check=True)

        def act(out, in_, func, reads, writes, bias=None, scale=None):
            kw = {}
            if bias is not None:
                kw["bias"] = bias
            if scale is not None:
                kw["scale"] = scale
            S.op("act", lambda e: e.activation(out=out, in_=in_, func=func, **kw), reads, writes)

        def ts(eng, out, in0, s1, s2, op0, op1, reads, writes, accum=None):
            kw = {}
            if op1 is not None:
                kw["op1"] = op1
            if accum is not None:
                kw["accum_out"] = accum
            S.op(eng, lambda e: e.tensor_scalar(out=out, in0=in0, scalar1=s1, scalar2=s2, op0=op0, **kw), reads, writes)

        def tt(eng, out, in0, in1, op, reads, writes):
            S.op(eng, lambda e: e.tensor_tensor(out=out, in0=in0, in1=in1, op=op), reads, writes)

        def stt(out, in0, scalar, in1, op0, op1, reads, writes):
            S.op("dve", lambda e: e.scalar_tensor_tensor(out=out, in0=in0, scalar=scalar, in1=in1, op0=op0, op1=op1), reads, writes)

        def cp(eng, out, in_, reads, writes):
            if eng == "act":
                S.op("act", lambda e: e.copy(out=out, in_=in_), reads, writes)
            else:
                S.op(eng, lambda e: e.tensor_copy(out=out, in_=in_), reads, writes)

        wslot = [0]
        bwsc = {}

        def wload(key, l, pc, n=None):
            n = wnp[key][1]
            i = wslot[0] % NWB
            wslot[0] += 1
            dma(wbf[i][:, :n], wsc[key][l, pc], [bwsc[(key, l, pc)]], [bwbf[i]], q=dq())
            return wbf[i], bwbf[i]

        def precast():
            for l in range(CFG["layers"]):
                for key in ["up0", "dn0", "winf", "wtm", "wglu", "wout", "up1", "dn1"]:
                    npc, n = wnp[key]
                    for pc in range(npc):
                        i = wslot[0] % NWB
                        wslot[0] += 1
                        b = Buf(); bwsc[(key, l, pc)] = b
                        dma(wst[:, :n], wsrc[key][l, pc], [], [bwst], q="sp")
                        cp("pool" if pc % 2 else "act", wbf[i][:, :n], wst[:, :n], [bwst], [bwbf[i]])
                        dma(wsc[key][l, pc], wbf[i][:, :n], [bwbf[i]], [b], q="sp")

        dma(CST[:], cst[:, :], [], [bCST])
        dma(PTB[:], ptb[:, :], [], [bIDX])
        for r in range(4):
            cp("dve", I4[:, r * 128:(r + 1) * 128], identf, [bCST], [bI])
            cp("dve", I8[:, r * 8:(r + 1) * 8], CST[0:8, 0:8], [bCST], [bI])
        S.op("dve", lambda e: e.memset(Vb[:], 1.0), [], [bVb])
        S.op("dve", lambda e: e.memset(Vt[0][:], 1.0), [], [bVt[0]])
        S.op("dve", lambda e: e.memset(Vt[1][:], 1.0), [], [bVt[1]])
        cp("dve", IDF[:], PTB[:], [bIDX], [bIDX])

        def sincos(ang_ap, n, dsin, dcos, t0, t1, ti, rd, wr):
            for (dst, off) in ((dsin, 0.0), (dcos, 0.25)):
                ts("dve", t0, ang_ap, 1.0 / TWO_PI, off, ALU.mult, ALU.add, rd, wr)
                cp("dve", ti, t0, wr, wr)
                cp("dve", t1, ti, wr, wr)
                tt("dve", t0, t0, t1, ALU.subtract, wr, wr)
                stt(t1, t0, 0.5, t0, ALU.is_gt, ALU.subtract, wr, wr)
                act(dst, t1, AF.Sin, wr, wr, scale=-TWO_PI)

        def build_tables(tau_ap, nf_ap):
            A = lambda i: ARENA[:, i * 2048:(i + 1) * 2048]
            for s in range(16):
                ts("dve", ARENA[:, s * 128:(s + 1) * 128], tau_ap, ANG[:, s:s + 1], None, ALU.mult, None, [bSS, bCST], [bARENA])
                ts("dve", rdT[:, s * 128:(s + 1) * 128], nf_ap, R16[:, s:s + 1], None, ALU.mult, None, [bSS, bCST], [bTab])
            sincos(A(0), 2048, sinT[:], cosT[:], A(1), A(2), AI[:, 3 * 2048:4 * 2048], [bARENA], [bARENA, bTab])

        def layer_setup(l):
            dma(PRM[:], prm[l], [], [bPRM])
            dma(SS[:], sS[l], [], [bSS])
            dma(H0[:], h0[l], [], [bH0])
            act(STP[:], SS[:, 32:48], AF.Exp, [bSS], [bSS])
            tt("dve", R16[:], SS[:, 0:16], STP[:], ALU.mult, [bSS], [bSS])
            act(R16[:], R16[:], AF.Exp, [bSS], [bSS])
            tt("dve", ANG[:], SS[:, 16:32], STP[:], ALU.mult, [bSS], [bSS])
            A = lambda i: ARENA[:, i * 512:(i + 1) * 512]
            rw = [bARENA]
            for c in range(4):
                dma(ARENA[:, 0:2560], sB[l, c], [], rw)
                act(A(7), A(2), AF.Exp, rw, rw)
                tt("dve", A(8), A(0), A(7), ALU.mult, rw, rw)
                act(A(8), A(8), AF.Exp, rw, rw)
                tt("dve", A(9), A(1), A(7), ALU.mult, rw, rw)
                sincos(A(9), 512, A(10), A(11), A(12), A(13), AI[:, 14 * 512:15 * 512], rw, rw)
                tt("dve", A(11), A(11), A(8), ALU.mult, rw, rw)
                tt("dve", A(10), A(10), A(8), ALU.mult, rw, rw)
                ts("dve", A(11), A(11), -1.0, None, ALU.add, None, rw, rw)
                tt("dve", A(12), A(0), A(0), ALU.mult, rw, rw)
                tt("dve", A(13), A(1), A(1), ALU.mult, rw, rw)
                tt("dve", A(12), A(12), A(13), ALU.add, rw, rw)
                S.op("dve", lambda e: e.reciprocal(out=A(12), in_=A(12)), rw, rw)
                tt("dve", A(13), A(11), A(0), ALU.mult, rw, rw)
                tt("dve", A(14), A(10), A(1), ALU.mult, rw, rw)
                tt("dve", A(13), A(13), A(14), ALU.add, rw, rw)
                tt("dve", A(13), A(13), A(12), ALU.mult, rw, rw)
                tt("dve", A(14), A(10), A(0), ALU.mult, rw, rw)
                tt("dve", A(15), A(11), A(1), ALU.mult, rw, rw)
                tt("dve", A(14), A(14), A(15), ALU.subtract, rw, rw)
                tt("dve", A(14), A(14), A(12), ALU.mult, rw, rw)
                cs_ = slice(c * 512, (c + 1) * 512)
                tt("dve", A(7), A(13), A(3), ALU.mult, rw, rw)
                tt("dve", A(8), A(14), A(4), ALU.mult, rw, rw)
                tt("dve", BBre[:, cs_], A(7), A(8), ALU.subtract, rw, [bSSM])
                tt("dve", A(7), A(13), A(4), ALU.mult, rw, rw)
                tt("dve", A(8), A(14), A(3), ALU.mult, rw, rw)
                tt("dve", BBim[:, cs_], A(7), A(8), ALU.add, rw, [bSSM])
            dma(ARENA[:, 0:4096], sC[l], [], rw)
            cp("dve", Cre[:], ARENA[:, 0:2048], rw, [bSSM])
            ts("dve", Cimn[:], ARENA[:, 2048:4096], -1.0, None, ALU.mult, None, rw, [bSSM])
            for c in range(4):
                ts("dve", diagD[:, c * 128:(c + 1) * 128], identf, PRM[:, 52 + c:53 + c], None, ALU.mult, None, [bPRM, bCST], [bSSM])
            S.op("dve", lambda e: e.memset(hpre[:], 0.0), [], [bhp])
            S.op("dve", lambda e: e.memset(hpim[:], 0.0), [], [bhp])
            for s in range(16):
                ts("dve", RH0[:, s * 16:(s + 1) * 16], H0[:, s * 16:(s + 1) * 16], R16[:, s:s + 1], None, ALU.mult, None, [bH0, bSS], [bH0])
                ts("dve", RH0[:, 256 + s * 16:256 + (s + 1) * 16], H0[:, 256 + s * 16:256 + (s + 1) * 16], R16[:, s:s + 1], None, ALU.mult, None, [bH0, bSS], [bH0])
            build_tables(taur, nfp)

        def layernorm(l, idx, N):
            pm, pv = ps[6][:, :N], ps[7][:, :N]
            S.op("pe", [mm(pm, onesf, xT[:, kt, :N], kt == 0, kt == KT - 1) for kt in range(KT)], [bxT, bCST], [bps[6]])
            for kt in range(KT):
                tt("dve", xT[:, kt, :N], xT[:, kt, :N], pm, ALU.subtract, [bxT, bps[6]], [bxT])
            for kt in range(KT):
                w_ = W[kt % 2]
                act(w_[:, :N], xT[:, kt, :N], AF.Square, [bxT], [bW[kt % 2]])
                S.op("pe", [mm(pv, onesf, w_[:, :N], kt == 0, kt == KT - 1)], [bW[kt % 2], bCST], [bps[7]])
            ts("dve", W[2][:, :N], pv, EPS2, None, ALU.add, None, [bps[7]], [bW[2]])
            act(W[2][:, :N], W[2][:, :N], AF.Sqrt, [bW[2]], [bW[2]])
            S.op("dve", lambda e: e.reciprocal(out=W[2][:, :N], in_=W[2][:, :N]), [bW[2]], [bW[2]])
            for kt in range(KT):
                tt("dve", xT[:, kt, :N], xT[:, kt, :N], W[2][:, :N], ALU.mult, [bxT, bW[2]], [bxT])
                act(xT[:, kt, :N], xT[:, kt, :N], AF.Identity, [bxT, bPRM], [bxT],
                    bias=PRM[:, 24 + idx * 8 + kt:25 + idx * 8 + kt], scale=PRM[:, idx * 8 + kt:idx * 8 + kt + 1])
                cp("pool", xb[:, kt, :N], xT[:, kt, :N], [bxT], [bxb])

        def ffn(l, which, N):
            cff = 0.5 / ALPHA
            for j in range(FT):
                wb, bw = wload("up%d" % which, l, j)
                pg, pu = ps[j % 2], ps[2 + j % 2]
                S.op("pe", [mm(pg[:, :N], wb[:, kt * 256:kt * 256 + 128], xb[:, kt, :N], kt == 0, kt == KT - 1) for kt in range(KT)],
                     [bw, bxb], [bps[j % 2]])
                S.op("pe", [mm(pu[:, :N], wb[:, kt * 256 + 128:kt * 256 + 256], xb[:, kt, :N], kt == 0, kt == KT - 1) for kt in range(KT)],
                     [bw, bxb], [bps[2 + j % 2]])
                w_ = W[j % 2]
                act(w_[:, :N], pg[:, :N], AF.Silu, [bps[j % 2]], [bW[j % 2]])
                tt("dve", HB[:, j * NC:j * NC + N], w_[:, :N], pu[:, :N], ALU.mult, [bW[j % 2], bps[2 + j % 2]], [bARENA])
            for i in range(KT):
                pd = ps[4 + i % 2]
                for hf in range(2):
                    wb, bw = wload("dn%d" % which, l, i * 2 + hf)
                    S.op("pe", [mm(pd[:, :N], wb[:, k * 128:(k + 1) * 128], HB[:, (hf * 11 + k) * NC:(hf * 11 + k) * NC + N],
                                   hf == 0 and k == 0, hf == 1 and k == 10) for k in range(11)], [bw, bARENA], [bps[4 + i % 2]])
                stt(xT[:, i, :N], pd[:, :N], cff, xT[:, i, :N], ALU.mult, ALU.add, [bps[4 + i % 2], bxT], [bxT])

        def w_in(l, N, c0, sample):
            kTo, kio, vo = (kTs, kiTs, vs) if sample else (kTp, kiTp, vp)
            for pc in range(10):
                wb, bw = wload("winf", l, pc)
                for tt_ in range(2):
                    t = pc * 2 + tt_
                    if t > 18 or t >= CFG["ntile"]:
                        continue
                    p_, bp_ = ps[t % 4], bps[t % 4]
                    S.op("pe", [mm(p_[:, :N], wb[:, tt_ * 1024 + kt * 128:tt_ * 1024 + (kt + 1) * 128], xb[:, kt, :N], kt == 0, kt == KT - 1)
                                for kt in range(KT)], [bw, bxb], [bp_])
                    if t < 3:
                        cp("act", wab[:, t, :N], p_[:, :N], [bp_], [bwab])
                        stt(wab[:, t, :N], wab[:, t, :N], -1.0, wab[:, t, :N], ALU.mult, ALU.max, [bwab], [bwab])
                    elif t < 7:
                        cp("act", uT[:, t - 3, :N], p_[:, :N], [bp_], [buT])
                    elif t < 11:
                        cp("act", qT[:, t - 7, :N], p_[:, :N], [bp_], [bqT])
                    elif t < 15:
                        cp("act", kT[:, t - 11, :N], p_[:, :N], [bp_], [bkT])
                        ts("dve", kst[:, :N], p_[:, :N], 1.0, None, ALU.mult, None, [bp_], [bkst])
                        dma(kTo[l, (t - 11) * 128:(t - 10) * 128, c0:c0 + N], kst[:, :N], [bkst], [])
                    elif t < 18:
                        tt("dve", qip[:, t - 15, :N], p_[:, :N], wab[:, t - 15, :N], ALU.mult, [bp_, bwab], [bqip])
                    else:
                        cp("act", ki4[:, :N], p_[:, :N], [bp_], [bki4])
                        cp("dve", kif[:, :N], p_[0:32, :N], [bp_], [bkif])
                        dma(kio[l, :, c0:c0 + N], kif[0:32, :N], [bkif], [])
            if CFG["sub"] < 1:
                return
            wA, bA = wload("wtm", l, 0)
            wB, bB = wload("wtm", l, 1)
            nsub = 16 if sample else N // 128
            for sub in range(nsub):
                m = 8 if sample else 128
                cols = slice(sub * m, (sub + 1) * m)
                pV, pW = ps[4 + sub % 2], ps[6 + sub % 2]
                S.op("pe", [mm(pV[:m, :], xb[:, kt, cols], (wA if kt < 4 else wB)[:, (kt % 4) * 520:(kt % 4) * 520 + 512], kt == 0, kt == KT - 1)
                            for kt in range(KT)], [bA, bB, bxb], [bps[4 + sub % 2]])
                S.op("pe", [mm(pW[:m, 0:8], xb[:, kt, cols], (wA if kt < 4 else wB)[:, (kt % 4) * 520 + 512:(kt % 4) * 520 + 520], kt == 0, kt == KT - 1)
                            for kt in range(KT)], [bA, bB, bxb], [bps[6 + sub % 2]])
                cp("dve", vst[:m, :], pV[:m, :], [bps[4 + sub % 2]], [bvst])
                if sample:
                    dma(vo[l, sub * 8:(sub + 1) * 8, :], vst[:8, :], [bvst], [])
                    S.op("act", lambda e, pV=pV: e.copy(out=Vb[:8, 0, :].rearrange("p (h d) -> p h d", d=65)[:, :, 0:64],
                                                 in_=pV[:8, :].rearrange("p (h d) -> p h d", d=64)), [bps[4 + sub % 2]], [bVb])
                    dma(VSh[sub * 8:(sub + 1) * 8, :], Vb[:8, 0, :], [bVb], [bVSh])
                    act(sgnS[:, sub, :], pW[:8, 0:8], AF.Sign, [bps[6 + sub % 2]], [bsgnS])
                else:
                    dma(vo[l, c0 + sub * 128:c0 + (sub + 1) * 128, :], vst[:, :], [bvst], [])
                    S.op("act", lambda e, pV=pV, sub=sub: e.copy(out=Vb[:, sub, :].rearrange("p (h d) -> p h d", d=65)[:, :, 0:64],
                                                          in_=pV[:, :].rearrange("p (h d) -> p h d", d=64)), [bps[4 + sub % 2]], [bVb])
                    act(sgn[:, sub, :], pW[:, 0:8], AF.Sign, [bps[6 + sub % 2]], [bsgn])
            if CFG["sub"] < 2:
                return
            if not sample:
                dma(kTh[:, c0:c0 + N].rearrange("(i p) n -> p i n", p=128), kT[:, :, :N], [bkT], [bhist])
                dma(kih[:, c0:c0 + N], ki4[:, :N], [bki4], [bhist])
                dma(Vh[c0:c0 + N, :].rearrange("(s p) f -> p s f", p=128), Vb[:, 0:N // 128, :], [bVb], [bhist])

        bhist = Buf(); bVSh = Buf()

        def ssm(l, N, sample):
            nsc = N // 128
            for c in range(4):
                yp, byp = ps[4 + c % 2], bps[4 + c % 2]
                for sc in range(nsc):
                    t0 = sc * 128
                    bre, bim = ps[0 + (sc % 2) * 2], ps[1 + (sc % 2) * 2]
                    bbr, bbi = bps[0 + (sc % 2) * 2], bps[1 + (sc % 2) * 2]
                    S.op("pe", [mm(bre[:, j * 128:(j + 1) * 128], BBre[:, (c * 4 + j) * 128:(c * 4 + j + 1) * 128], uT[:, c, t0:t0 + 128], j == 0, j == 3)
                                for j in range(4)], [bSSM, buT], [bbr])
                    S.op("pe", [mm(bim[:, j * 128:(j + 1) * 128], BBim[:, (c * 4 + j) * 128:(c * 4 + j + 1) * 128], uT[:, c, t0:t0 + 128], j == 0, j == 3)
                                for j in range(4)], [bSSM, buT], [bbi])
                    cs = cosT[:, c * 512:(c + 1) * 512]; sn = sinT[:, c * 512:(c + 1) * 512]
                    tt("dve", W[2][:], bre, cs, ALU.mult, [bbr, bTab], [bW[2]])
                    tt("dve", W[3][:], bim, sn, ALU.mult, [bbi, bTab], [bW[3]])
                    tt("dve", W[0][:], W[2][:], W[3][:], ALU.add, [bW[2], bW[3]], [bW[0]])
                    tt("dve", W[2][:], bim, cs, ALU.mult, [bbi, bTab], [bW[2]])
                    tt("dve", W[3][:], bre, sn, ALU.mult, [bbr, bTab], [bW[3]])
                    tt("dve", W[1][:], W[2][:], W[3][:], ALU.subtract, [bW[2], bW[3]], [bW[1]])
                    if sample:
                        for (w_, bw_, off) in ((W[0], bW[0], 0), (W[1], bW[1], 256)):
                            v = w_[:, :].rearrange("p (j s t) -> p j s t", j=4, s=16)[:, :, :, 0]
                            r_ = RH0[:, off + c * 64:off + (c + 1) * 64].rearrange("p (j s) -> p j s", j=4)
                            tt("dve", v, v, r_, ALU.add, [bw_, bH0], [bw_])
                    for j in range(4):
                        sl = slice(j * 128, (j + 1) * 128)
                        s_ = c * 4 + j
                        ini_re = 0.0 if sample else hpre[:, s_:s_ + 1]
                        ini_im = 0.0 if sample else hpim[:, s_:s_ + 1]
                        S.op("dve", lambda e, sl=sl, s_=s_, ini=ini_re: e.tensor_tensor_scan(out=W[2][:, sl], data0=rdT[:, s_ * 128:(s_ + 1) * 128], data1=W[0][:, sl],
                                                                                       initial=ini, op0=ALU.mult, op1=ALU.add), [bW[0], bTab, bhp], [bW[2]])
                        S.op("dve", lambda e, sl=sl, s_=s_, ini=ini_im: e.tensor_tensor_scan(out=W[3][:, sl], data0=rdT[:, s_ * 128:(s_ + 1) * 128], data1=W[1][:, sl],
                                                                                       initial=ini, op0=ALU.mult, op1=ALU.add), [bW[1], bTab, bhp], [bW[3]])
                    tt("pool", W[4][:], W[2][:], cs, ALU.mult, [bW[2], bTab], [bW[4]])
                    tt("pool", W[5][:], W[3][:], sn, ALU.mult, [bW[3], bTab], [bW[5]])
                    tt("dve", W[0][:], W[4][:], W[5][:], ALU.subtract, [bW[4], bW[5]], [bW[0]])
                    tt("pool", W[4][:], W[3][:], cs, ALU.mult, [bW[3], bTab], [bW[4]])
                    tt("pool", W[5][:], W[2][:], sn, ALU.mult, [bW[2], bTab], [bW[5]])
                    tt("dve", W[1][:], W[4][:], W[5][:], ALU.add, [bW[4], bW[5]], [bW[1]])
                    if sample:
                        for (w_, bw_, off) in ((W[0], bW[0], 0), (W[1], bW[1], 256)):
                            v = w_[:, :].rearrange("p (j s t) -> p j s t", j=4, s=16)[:, :, :, 7]
                            o_ = HSo[:, off + c * 64:off + (c + 1) * 64].rearrange("p (j s) -> p j s", j=4)
                            cp("dve", o_, v, [bw_], [bHS])
                    else:
                        cp("dve", hpre[:, c * 4:c * 4 + 4], W[0][:, :].rearrange("p (j t) -> p j t", j=4)[:, :, 127], [bW[0]], [bhp])
                        cp("dve", hpim[:, c * 4:c * 4 + 4], W[1][:, :].rearrange("p (j t) -> p j t", j=4)[:, :, 127], [bW[1]], [bhp])
                    cp("act", hbre[:], W[0][:], [bW[0]], [bhb])
                    cp("act", hbim[:], W[1][:], [bW[1]], [bhb])
                    fns = []
                    for j in range(4):
                        s_ = c * 4 + j
                        fns.append(mm(yp[:, t0:t0 + 128], Cre[:, s_ * 128:(s_ + 1) * 128], hbre[:, j * 128:(j + 1) * 128], sc == 0 and j == 0, False))
                        fns.append(mm(yp[:, t0:t0 + 128], Cimn[:, s_ * 128:(s_ + 1) * 128], hbim[:, j * 128:(j + 1) * 128], False, False))
                    fns.append(mm(yp[:, t0:t0 + 128], diagD[:, c * 128:(c + 1) * 128], uT[:, c, t0:t0 + 128], False, sc == nsc - 1))
                    S.op("pe", fns, [bSSM, bhb, buT], [byp])
                act(W[4][:, :N], yp[:, :N], AF.Identity, [byp], [bW[4]], scale=0.5)
                act(W[5][:, :N], yp[:, :N], AF.Square, [byp], [bW[5]])
                ts("dve", W[5][:, :N], W[5][:, :N], 2.0 * GK * 0.044715, 2.0 * GK, ALU.mult, ALU.add, [bW[5]], [bW[5]])
                tt("dve", W[5][:, :N], W[5][:, :N], W[4][:, :N], ALU.mult, [bW[5], bW[4]], [bW[5]])
                act(W[5][:, :N], W[5][:, :N], AF.Tanh, [bW[5]], [bW[5]])
                stt(gT[:, c, :N], W[5][:, :N], 1.0, W[4][:, :N], ALU.add, ALU.mult, [bW[5], bW[4]], [bgT])
            wb, bw = wload("wglu", l, 0)
            for m in range(4):
                p_, bp_ = ps[m % 2], bps[m % 2]
                S.op("pe", [mm(p_[:, :N], wb[:, (m * 4 + kt) * 128:(m * 4 + kt + 1) * 128], gT[:, kt, :N], kt == 0, kt == 3) for kt in range(4)], [bw, bgT], [bp_])
                act(W[m % 2][:, :N], p_[:, :N], AF.Sigmoid, [bp_, bPRM], [bW[m % 2]], bias=PRM[:, 48 + m:49 + m])
                tt("dve", ysT[:, m, :N], gT[:, m, :N], W[m % 2][:, :N], ALU.mult, [bgT, bW[m % 2]], [bysT])

        rot = {"k": 0, "v": 0, "kit": 0, "rh": 0, "mb": 0, "pt": 0, "x": 0}

        def qblock(nq, qsl, sgn_ap, bsg, stiles, kblocks, nk, tri_ap, dst):
            I_ = I4 if nq == 128 else I8
            for h in range(8):
                ts("pool", sd[:nq, h, :nq], identf[:nq, :nq], sgn_ap[:, h:h + 1], None, ALU.mult, None, [bCST, bsg], [bsd])
            off = 0
            for (w, get) in stiles:
                kt_, bk_ = get()
                sp_, bsp_ = ps[3], bps[3]
                for h in range(8):
                    xi = rot["x"] % 3; rot["x"] += 1
                    ri = rot["rh"] % 3; rot["rh"] += 1
                    j = h % 3
                    S.op("pe", [mm(ps[xi][:nq, :w], qip[32 * j:32 * j + 32, h // 3, qsl], kt_[32 * j:32 * j + 32, :w], True, True)], [bqip, bk_], [bps[xi]])
                    act(Rh[ri][:nq, :w], ps[xi][:nq, :w], AF.Relu, [bps[xi]], [bRh[ri]])
                    S.op("pe", [mm(sp_[:nq, :w], sd[:nq, h, :nq], Rh[ri][:nq, :w], h == 0, h == 7)], [bsd, bRh[ri]], [bsp_])
                cp("dve", ARENA[:nq, off:off + w], sp_[:nq, :w], [bsp_], [bARENA])
                off += w
            kwl = kblocks[-1][0]
            tt("dve", ARENA[:nq, nk - kwl:nk], ARENA[:nq, nk - kwl:nk], tri_ap, ALU.add, [bARENA, bCST], [bARENA])
            S.op("dve", lambda e: e.memset(lo[:], -2048.0), [], [bbis])
            if nk > 256:
                for it in range(NITER):
                    wk = 2048.0 / (2.0 ** it)
                    ts("dve", mid[:nq], lo[:nq], wk, None, ALU.add, None, [bbis], [bbis])
                    npc = (nk + 2047) // 2048
                    for pc in range(npc):
                        a, b = pc * 2048, min(nk, (pc + 1) * 2048)
                        s2 = None if pc == 0 else cnt[:nq, (pc - 1) % 2:(pc - 1) % 2 + 1]
                        ts("dve", junk[:nq, :b - a], ARENA[:nq, a:b], mid[:nq], s2, ALU.is_gt, ALU.add, [bARENA, bbis], [bjunk, bbis],
                           accum=cnt[:nq, pc % 2:pc % 2 + 1])
                    cl = cnt[:nq, (npc - 1) % 2:(npc - 1) % 2 + 1]
                    ts("dve", tq[:nq], cl, 255.5, wk, ALU.is_gt, ALU.mult, [bbis], [bbis])
                    tt("dve", lo[:nq], lo[:nq], tq[:nq], ALU.add, [bbis], [bbis])
            O = [ps[6], ps[7]]; bO = [bps[6], bps[7]]
            LTs = [PS[:, 4 * 512:6 * 512], PS[:, 0:1024]]; bLTs = [[bps[4], bps[5]], [bps[0], bps[1]]]
            off = 0
            nb = len(kblocks)

            def emit_pv(bi, kw, pi, vv, bvv):
                fns = []
                for h in range(8):
                    hf, hh = h % 2, h // 2
                    fns.append(mm(O[h // 4][:nq, (h % 4) * 65:(h % 4) * 65 + 65], PT[pi][:kw, hf * 512 + hh * nq:hf * 512 + (hh + 1) * nq],
                                  vv[:kw, h * 65:h * 65 + 65], bi == 0 and h % 4 == 0, bi == nb - 1 and h % 4 == 3))
                S.op("pe", fns, [bPT[pi], bvv], bO)

            pending = None
            for bi, (kw, get) in enumerate(kblocks):
                kk, bkk, vv, bvv = get()
                mi = rot["mb"] % 2; rot["mb"] += 1
                pi = rot["pt"] % 2; rot["pt"] += 1
                LT, bLT = LTs[bi % 2], bLTs[bi % 2]
                ts("dve", mb[mi][:nq, :kw], ARENA[:nq, off:off + kw], lo[:nq], NEG, ALU.is_le, ALU.mult, [bARENA, bbis], [bmb[mi]])
                fns = []
                for hf in range(2):
                    fns.append(mm(LT[:kw, hf * 512:hf * 512 + 4 * nq], mb[mi][:nq, :kw], I_[:nq, :4 * nq], True, False))
                for hh in range(4):
                    for hf in range(2):
                        fns.append(mm(LT[:kw, hf * 512 + hh * nq:hf * 512 + (hh + 1) * nq], kk[64 * hf:64 * hf + 64, hh, :kw],
                                      qT[64 * hf:64 * hf + 64, hh, qsl], False, hh == 3))
                S.op("pe", fns, [bmb[mi], bI, bkk, bqT], bLT)
                for hf in range(2):
                    act(PT[pi][:kw, hf * 512:hf * 512 + 4 * nq], LT[:kw, hf * 512:hf * 512 + 4 * nq], AF.Exp, bLT, [bPT[pi]], scale=0.125)
                if pending is not None:
                    emit_pv(*pending)
                pending = (bi, kw, pi, vv, bvv)
                off += kw
            emit_pv(*pending)
            for h in range(8):
                o_ = O[h // 4]
                S.op("dve", lambda e, o_=o_, h=h: e.reciprocal(out=rdn[:nq, h:h + 1], in_=o_[:nq, (h % 4) * 65 + 64:(h % 4) * 65 + 65]), bO, [brdn])
                ts("dve", yat[:nq, h * 64:(h + 1) * 64], o_[:nq, (h % 4) * 65:(h % 4) * 65 + 64], rdn[:nq, h:h + 1], None, ALU.mult, None, bO + [brdn], [byat])
            for i in range(4):
                xi = rot["x"] % 3; rot["x"] += 1
                S.op("pe", [lambda e, xi=xi, i=i: e.transpose(out=ps[xi][:, :nq], in_=yat[:nq, i * 128:(i + 1) * 128], identity=identf[:nq, :nq])], [byat, bCST], [bps[xi]])
                cp("act", dst(i), ps[xi][:, :nq], [bps[xi]], [byattT])

        def prompt_attention(N, c0):
            for sub in range(N // 128):
                qb = (c0 + sub * 128) // 128
                nk = 128 * (qb + 1)
                stiles = []
                for st in range((nk + 511) // 512):
                    w = min(512, nk - st * 512)

                    def get(st=st, w=w):
                        i = rot["kit"] % 2; rot["kit"] += 1
                        dma(kit[i][:, :w], kih[:, st * 512:st * 512 + w], [bhist], [bkit[i]], q=dq())
                        return kit[i], bkit[i]
                    stiles.append((w, get))
                kblocks = []
                for sb_ in range(qb + 1):
                    def getk(sb_=sb_):
                        i = rot["k"] % 2; rot["k"] += 1
                        dma(kTt[i][:, :, :], kTh[:, sb_ * 128:(sb_ + 1) * 128].rearrange("(i p) n -> p i n", p=128), [bhist], [bkTt[i]], q=dq())
                        dma(Vt[i][:, :], Vh[sb_ * 128:(sb_ + 1) * 128, :], [bhist], [bVt[i]], q=dq())
                        return kTt[i], bkTt[i], Vt[i], bVt[i]
                    kblocks.append((128, getk))
                qblock(128, slice(sub * 128, (sub + 1) * 128), sgn[:, sub, :], bsgn, stiles, kblocks, nk, tri,
                       lambda i, sub=sub: yattT[:, i, sub * 128:(sub + 1) * 128])

        def sample_attention(l):
            for seq in range(16):
                for pg in range(16):
                    S.dma("pool", lambda e, pg=pg, seq=seq: e.indirect_dma_start(
                        out=kip[:, pg, :], out_offset=None, in_=cki[:, :],
                        in_offset=bass.IndirectOffsetOnAxis(ap=IDX[:, seq * 16 + pg:seq * 16 + pg + 1], axis=0)), [bIDX], [bkip])
                for g in range(4):
                    xi = rot["x"] % 3; rot["x"] += 1
                    S.op("pe", [lambda e, xi=xi, g=g, k=k: e.transpose(out=ps[xi][0:32, k * 128:(k + 1) * 128], in_=kip[:, g * 4 + k, :], identity=identf)
                                for k in range(4)], [bkip, bCST], [bps[xi]])
                    for r in range(4):
                        cp("act" if r % 2 else "dve", kitS[32 * r:32 * r + 32, g * 512:(g + 1) * 512], ps[xi][0:32, :], [bps[xi]], [bkitS])
                cp("dve", kitS[:, 2048:2056], ki4[:, seq * 8:(seq + 1) * 8], [bki4], [bkitS])
                stiles = []
                for st in range(5):
                    w = 512 if st < 4 else 8
                    stiles.append((w, lambda st=st, w=w: (kitS[:, st * 512:st * 512 + w], bkitS)))
                kblocks = []
                for pg in range(16):
                    def getk(pg=pg, seq=seq):
                        i = rot["k"] % 2; rot["k"] += 1
                        S.dma("pool", lambda e: e.indirect_dma_start(out=kpg[i][:, :], out_offset=None, in_=ck[:, :],
                              in_offset=bass.IndirectOffsetOnAxis(ap=IDX[:, seq * 16 + pg:seq * 16 + pg + 1], axis=0)), [bIDX], [bkpg[i]])
                        S.dma("pool", lambda e: e.indirect_dma_start(out=vpg[i][:, :], out_offset=None, in_=cv[:, :],
                              in_offset=bass.IndirectOffsetOnAxis(ap=IDX[:, seq * 16 + pg:seq * 16 + pg + 1], axis=0)), [bIDX], [bvpg[i]])
                        xi = rot["x"] % 3; rot["x"] += 1
                        S.op("pe", [lambda e, k=k: e.transpose(out=ps[xi][:, k * 128:(k + 1) * 128], in_=kpg[i][:, k * 128:(k + 1) * 128], identity=identf)
                                    for k in range(4)], [bkpg[i], bCST], [bps[xi]])
                        cp("act", kTt[i][:, :, :].rearrange("p i n -> p (i n)"), ps[xi][:, :], [bps[xi]], [bkTt[i]])
                        S.op("pool", lambda e: e.tensor_copy(out=Vt[i][:, :].rearrange("p (h d) -> p h d", d=65)[:, :, 0:64],
                                                      in_=vpg[i][:, :].rearrange("p (h d) -> p h d", d=64)), [bvpg[i]], [bVt[i]])
                        return kTt[i], bkTt[i], Vt[i], bVt[i]
                    kblocks.append((128, getk))

                def getn(seq=seq):
                    i = rot["k"] % 2; rot["k"] += 1
                    cp("dve", kTt[i][:, :, 0:8], kT[:, :, seq * 8:(seq + 1) * 8], [bkT], [bkTt[i]])
                    dma(Vt[i][0:8, :], VSh[seq * 8:(seq + 1) * 8, :], [bVSh], [bVt[i]])
                    return kTt[i], bkTt[i], Vt[i], bVt[i]
                kblocks.append((8, getn))
                qblock(8, slice(seq * 8, (seq + 1) * 8), sgnS[:, seq, :], bsgnS, stiles, kblocks, 2056, CST[0:8, 128:136],
                       lambda i, seq=seq: yattT[:, i, seq * 8:(seq + 1) * 8])

        def w_out(l, N):
            for pc in range(4):
                wb, bw = wload("wout", l, pc)
                for mm_ in range(2):
                    i = pc * 2 + mm_
                    p_, bp_ = ps[i % 2], bps[i % 2]
                    S.op("pe", [mm(p_[:, :N], wb[:, mm_ * 1024 + kt * 128:mm_ * 1024 + (kt + 1) * 128], (ysT if kt < 4 else yattT)[:, kt % 4, :N], kt == 0, kt == KT - 1)
                                for kt in range(KT)], [bw, bysT, byattT], [bp_])
                    stt(xT[:, i, :N], p_[:, :N], 1.0 / ALPHA, xT[:, i, :N], ALU.mult, ALU.add, [bp_, bxT], [bxT])

        bx1 = [Buf() for _ in range(NCH + 1)]
        ST = CFG["stop"]
        precast()
        for l in range(CFG["layers"]):
            if ST > 0:
                layer_setup(l)
            for ci in list(range(CFG["nch"])) + ([NCH] if CFG["sample"] else []):
                sample = ci == NCH
                N = NS if sample else NC
                c0 = 0 if sample else ci * NC
                if sample:
                    build_tables(taus, nfs)
                    cp("dve", IDF[:], PTB[:], [bIDX], [bIDX])
                    ts("dve", IDF[:], IDF[:], 128.0, float(l * 2560 * 128), ALU.mult, ALU.add, [bIDX], [bIDX])
                    ts("dve", IDF[:], IDF[:], iotaf, None, ALU.add, None, [bIDX, bCST], [bIDX])
                    cp("dve", IDX[:], IDF[:], [bIDX], [bIDX])
                src = (xTs if sample else xTp) if l == 0 else (x1s if sample else x1p)
                if ST > 1:
                    dma(xT[:, :, :N], src[:, c0:c0 + N].rearrange("(k p) n -> p k n", p=128), [bx1[ci]], [bxT])
                    for kt in range(KT):
                        cp("pool", xb[:, kt, :N], xT[:, kt, :N], [bxT], [bxb])
                if ST > 2:
                    ffn(l, 0, N)
                if ST > 3:
                    layernorm(l, 0, N)
                if ST > 4:
                    w_in(l, N, c0, sample)
                if ST > 5:
                    ssm(l, N, sample)
                if ST > 6:
                    if sample:
                        sample_attention(l)
                        dma(hs[l], HSo[:], [bHS], [])
                    else:
                        prompt_attention(N, c0)
                if ST > 7:
                    w_out(l, N)
                    layernorm(l, 1, N)
                    ffn(l, 1, N)
                    layernorm(l, 2, N)
                dst = (x1s if sample else x1p) if l == 0 else (yTs if sample else yTp)
                if ST > 1:
                    dma(dst[:, c0:c0 + N].rearrange("(k p) n -> p k n", p=128), xT[:, :, :N], [bxT], [bx1[ci]])
                if ci == CFG["nch"] - 1:
                    dma(hp[l, :, 0:16], hpre[:], [bhp], [])
                    dma(hp[l, :, 16:32], hpim[:], [bhp], [])
        S.finish()

        with nc.Block() as block:
            @block.tensor
            def _(e):
                for f in S.ops["pe"]:
                    f(e)

            @block.scalar
            def _(e):
                for f in S.ops["act"]:
                    f(e)

            @block.vector
            def _(e):
                for f in S.ops["dve"]:
                    f(e)

            @block.gpsimd
            def _(e):
                for f in S.ops["pool"]:
                    f(e)

            @block.sync
            def _(e):
                for f in S.ops["sp"]:
                    f(e)
    return nc, {e: len(v) for e, v in S.ops.items()}


def _prep(inp):
    f = np.float32
    g = {}
    def C(a):
        return np.ascontiguousarray(a, dtype=f)
    for w, name in ((0, "ffn1_up"), (1, "ffn2_up")):
        W = inp[name].reshape(L, KT, 128, 2, FT, 128)
        g["up%d" % (w + 1)] = C(W.transpose(0, 4, 2, 1, 3, 5).reshape(L, FT, 128, 2048))
    for w, name in ((0, "ffn1_down"), (1, "ffn2_down")):
        W = inp[name].reshape(L, 2, 11, 128, 8, 128)
        g["dn%d" % (w + 1)] = C(W.transpose(0, 4, 1, 3, 2, 5).reshape(L, 16, 128, 1408))
    win = inp["w_in"]
    u, q, k, v = win[:, :, 0:512], win[:, :, 512:1024], win[:, :, 1024:1536], win[:, :, 1536:2048]
    qi, ki, wi = win[:, :, 2048:2304], win[:, :, 2304:2336], win[:, :, 2336:2344]
    wexp = np.repeat(wi, 32, axis=2)
    ki4 = np.tile(ki, (1, 1, 4))
    qi3 = np.zeros((L, D, 384), f); we3 = np.zeros((L, D, 384), f)
    for h in range(8):
        o = (h // 3) * 128 + (h % 3) * 32
        qi3[:, :, o:o + 32] = qi[:, :, h * 32:(h + 1) * 32]
        we3[:, :, o:o + 32] = wexp[:, :, h * 32:(h + 1) * 32]
    fm = np.concatenate([we3, u, q, k, qi3, ki4, np.zeros((L, D, 128), f)], axis=2)
    fm = fm.reshape(L, KT, 128, 10, 2, 128)
    g["winf"] = C(fm.transpose(0, 3, 2, 4, 1, 5).reshape(L, 10, 128, 2048))
    tm = np.concatenate([v, wi], axis=2).reshape(L, 2, 4, 128, 520)
    g["wtm"] = C(tm.transpose(0, 1, 3, 2, 4).reshape(L, 2, 128, 2080))
    g["wglu"] = C(inp["w_glu"].reshape(L, 4, 128, 4, 128).transpose(0, 2, 3, 1, 4).reshape(L, 128, 2048))
    wo = inp["w_out"].reshape(L, KT, 128, 4, 2, 128)
    g["wout"] = C(wo.transpose(0, 3, 2, 4, 1, 5).reshape(L, 4, 128, 2048))
    prm = np.zeros((L, 128, 56), f)
    prm[:, :, 0:24] = inp["ln_g"].reshape(L, 3, KT, 128).transpose(0, 3, 1, 2).reshape(L, 128, 24)
    prm[:, :, 24:48] = inp["ln_b"].reshape(L, 3, KT, 128).transpose(0, 3, 1, 2).reshape(L, 128, 24)
    prm[:, :, 48:52] = inp["b_glu"].reshape(L, 4, 128).transpose(0, 2, 1)
    prm[:, :, 52:56] = inp["ssm_d"].reshape(L, 4, 128).transpose(0, 2, 1)
    g["prm"] = prm
    st = lambda a: a.reshape(L, 16, 128).transpose(0, 2, 1)
    lst = np.repeat(inp["ssm_log_step"][:, :, None], 64, axis=2)
    g["sS"] = C(np.concatenate([st(inp["ssm_lambda_re"]), st(inp["ssm_lambda_im"]), st(lst)], axis=2))
    def bbl(a):
        x = a.reshape(L, 4, 4, 128)
        x = x.transpose(0, 2, 1, 3)[:, :, None]
        return np.broadcast_to(x, (L, 4, 32, 4, 128)).reshape(L, 128, 512)
    def bexp(b):
        o = np.zeros((L, 4, 2, 16, 4, 2, 64), f)
        bb = b.reshape(L, 4, 4, 2, 64, 16)
        for g2 in range(2):
            o[:, :, g2, :, :, g2, :] = bb[:, :, :, g2].transpose(0, 2, 4, 1, 3)
        return o.reshape(L, 128, 512)
    def zl(a):
        x = a.reshape(L, 4, 1, 512)
        return np.broadcast_to(x, (L, 4, 128, 512))
    def bz(be):
        x = be.reshape(L, 4, 32, 4, 128)
        o = np.zeros((L, 4, 4, 32, 4, 128), f)
        for j in range(4):
            o[:, :, j, :, j, :] = x[:, j].transpose(0, 2, 1, 3)
        return o.reshape(L, 4, 128, 512)
    g["sB"] = C(np.concatenate([zl(inp["ssm_lambda_re"].reshape(L, 16, 128)), zl(inp["ssm_lambda_im"].reshape(L, 16, 128)),
                                zl(lst.reshape(L, 16, 128)), bz(bexp(inp["ssm_b_re"])), bz(bexp(inp["ssm_b_im"]))], axis=3))
    def cexp(cm):
        o = np.zeros((L, 2, 64, 16, 4, 2, 16), f)
        cc = cm.reshape(L, 16, 2, 16, 64)
        for s in range(16):
            for g2 in range(2):
                o[:, g2, :, s, s % 4, g2, :] = cc[:, s, g2].transpose(0, 2, 1)
        return o.reshape(L, 128, 2048)
    g["sC"] = C(np.concatenate([cexp(inp["ssm_c_re"]), cexp(inp["ssm_c_im"])], axis=2))
    ck = inp["cache_k"].reshape(L * 2560 * 128, 512)
    cv = inp["cache_v"].reshape(L * 2560 * 128, 512)
    cki = inp["cache_kidx"].reshape(L * 2560 * 128, 32)
    cst = np.zeros((128, 897), f)
    cst[:, 0:128] = np.eye(128)
    cst[:, 128:256] = np.where(np.arange(128)[None, :] > np.arange(128)[:, None], NEG, 0.0)
    cst[:, 256:384] = np.arange(1, 129)[None, :]
    cst[:, 384:512] = np.tile(np.arange(1, 9), 16)[None, :]
    cst[:, 512:640] = 1.0; cst[:, 512] = 0.0; cst[:, 512:640][:, 0] = 1.0
    nfs = np.ones(128, f); nfs[0::8] = 0.0
    cst[:, 640:768] = nfs[None, :]
    cst[:, 768:896] = 1.0 / 1024.0
    cst[:, 896] = np.arange(128)
    g["cst"] = cst
    cores = []
    for c in range(8):
        b = c % 2
        d = dict(g)
        d["xTp"] = C(inp["x_prompt"][b].T)
        d["xTs"] = C(inp["x_sample"][16 * c:16 * c + 16].reshape(128, D).T)
        h0 = np.zeros((L, 128, 512), f)
        for i, nm in enumerate(("state_ssm_re", "state_ssm_im")):
            a = inp[nm][:, 16 * c:16 * c + 16].reshape(L, 16, 16, 128)
            h0[:, :, i * 256:(i + 1) * 256] = a.transpose(0, 3, 2, 1).reshape(L, 128, 256)
        d["h0"] = h0
        d["ck"], d["cv"], d["cki"] = ck, cv, cki
        d["ptb"] = np.ascontiguousarray(np.broadcast_to(inp["page_table"][16 * c:16 * c + 16].reshape(1, 256), (128, 256)).astype(np.int32))
        cores.append(d)
    return cores


_NC = None


def kernel(**inputs):
    global _NC
    inp = {k: np.asarray(v) for k, v in inputs.items()}
    if _NC is None:
        _NC = build()[0]
    in_maps = _prep(inp)
    res = run_bass_kernel_spmd(_NC, in_maps, core_ids=list(range(8))).results
    f = np.float32
    B = 2
    y_p = np.stack([res[b]["yTp"].T for b in range(B)]).astype(f)
    y_s = np.concatenate([res[c]["yTs"].T.reshape(16, 8, D) for c in range(8)]).astype(f)
    k_p = np.stack([np.stack([res[b]["kTp"][l].T.reshape(NP, 8, 64) for b in range(B)]) for l in range(L)]).astype(f)
    v_p = np.stack([np.stack([res[b]["vp"][l].reshape(NP, 8, 64) for b in range(B)]) for l in range(L)]).astype(f)
    ki_p = np.stack([np.stack([res[b]["kiTp"][l].T for b in range(B)]) for l in range(L)]).astype(f)
    def hst(a):
        return a.T.reshape(32, 64)
    hr_p = np.stack([np.stack([hst(res[b]["hp"][l][:, 0:16]) for b in range(B)]) for l in range(L)]).astype(f)
    hi_p = np.stack([np.stack([hst(res[b]["hp"][l][:, 16:32]) for b in range(B)]) for l in range(L)]).astype(f)
    k_s = np.stack([np.concatenate([res[c]["kTs"][l].T.reshape(16, 8, 8, 64) for c in range(8)]) for l in range(L)]).astype(f)
    v_s = np.stack([np.concatenate([res[c]["vs"][l].reshape(16, 8, 8, 64) for c in range(8)]) for l in range(L)]).astype(f)
    ki_s = np.stack([np.concatenate([res[c]["kiTs"][l].T.reshape(16, 8, 32) for c in range(8)]) for l in range(L)]).astype(f)
    def hss(a):
        return a.reshape(128, 16, 16).transpose(2, 1, 0).reshape(16, 32, 64)
    hr_s = np.stack([np.concatenate([hss(res[c]["hs"][l][:, 0:256]) for c in range(8)]) for l in range(L)]).astype(f)
    hi_s = np.stack([np.concatenate([hss(res[c]["hs"][l][:, 256:512]) for c in range(8)]) for l in range(L)]).astype(f)
    return (y_p, y_s, k_p, v_p, ki_p, hr_p, hi_p, k_s, v_s, ki_s, hr_s, hi_s)
```
